# Optimizing a Trainium2 kernel written in Bass

```python
import jax, jax.numpy as jnp
from jax import lax
import numpy as np

D_MODEL = 1024
BATCH = 2
SEQ = 8192
DEPTH = 2
DEC_BATCH = 8
DEC_SEQ = 4096
PAST_LEN = 128

GRID_W = 64
NA_HEAD_DIM = 32
NA_HEADS = D_MODEL // NA_HEAD_DIM
NA_ROWS = 8
NA_COLS = 16
NA_Q_COLS = 16
NA_K_COLS = NA_Q_COLS + NA_COLS
RWKV_HEAD_DIM = 64
RWKV_HEADS = D_MODEL // RWKV_HEAD_DIM
DECAY_LORA = 64
ICLR_LORA = 64
GATE_LORA = 160
GN_EPS = 64e-5
D_FF = 2816
N_EXPERTS = 8
TOP_K = 2
D_EXPERT = 3584
LN_EPS = 1e-5
N_EVEN = (DEPTH + 1) // 2
N_ODD = DEPTH // 2
ALPHA = (2 * DEPTH) ** 0.25
BETA = (8 * DEPTH) ** -0.25

kernel_name = 'hybrid_na_rwkv7_deepnorm_encoder'


def layer_norm(x, g, b):
    xf = x.astype(jnp.float32)
    xc = xf - jnp.mean(xf, axis=-1, keepdims=True)
    var = jnp.mean(xc * xc, axis=-1, keepdims=True)
    return (xc * lax.rsqrt(var + LN_EPS) * g.astype(jnp.float32) + b.astype(jnp.float32)).astype(x.dtype)


def _na_column_layout():
    n_cb = GRID_W // NA_Q_COLS
    q_cols = np.arange(GRID_W).reshape(n_cb, NA_Q_COLS)
    q_start = np.clip(q_cols - NA_COLS // 2, 0, GRID_W - NA_COLS)
    blk_start = np.clip(np.arange(n_cb) * NA_Q_COLS - NA_COLS // 2, 0, GRID_W - NA_K_COLS)
    key_cols = blk_start[:, None] + np.arange(NA_K_COLS)
    kc = key_cols[:, None, :]
    valid = (kc >= q_start[..., None]) & (kc < q_start[..., None] + NA_COLS)
    dc_idx = np.clip(kc - q_cols[..., None] + NA_COLS - 1, 0, 2 * NA_COLS - 2)
    return key_cols, valid, dc_idx


def neighbourhood_attention(x, w_qkv, rpb, w_o):
    b, l, d = x.shape
    rows = l // GRID_W
    kr = min(NA_ROWS, rows)
    n_cb = GRID_W // NA_Q_COLS
    key_cols, valid, dc_idx = _na_column_layout()
    qkv = jnp.einsum('bld,de->ble', x, w_qkv).reshape(b, rows, GRID_W, 3, NA_HEADS, NA_HEAD_DIM)
    q = qkv[:, :, :, 0] * (NA_HEAD_DIM ** -0.5)
    k = qkv[:, :, :, 1]
    v = qkv[:, :, :, 2]
    mask = np.broadcast_to(valid[:, :, None, :], (n_cb, NA_Q_COLS, kr, NA_K_COLS)).reshape(n_cb, NA_Q_COLS, kr * NA_K_COLS)

    def row_block(r):
        rs = jnp.clip(r - kr // 2, 0, rows - kr)
        q_r = lax.dynamic_index_in_dim(q, r, axis=1, keepdims=False).reshape(b, n_cb, NA_Q_COLS, NA_HEADS, NA_HEAD_DIM)

        def gather(t):
            t_r = lax.dynamic_slice_in_dim(t, rs, kr, axis=1)[:, :, key_cols]
            return jnp.moveaxis(t_r, 2, 1).reshape(b, n_cb, kr * NA_K_COLS, NA_HEADS, NA_HEAD_DIM)

        k_blk = gather(k)
        v_blk = gather(v)
        dr_idx = rs + jnp.arange(kr) - r + NA_ROWS - 1
        bias = rpb[:, dr_idx][:, :, dc_idx]
        bias = jnp.transpose(bias, (0, 2, 3, 1, 4)).reshape(NA_HEADS, n_cb, NA_Q_COLS, kr * NA_K_COLS)
        s = jnp.einsum('bcqhd,bckhd->bhcqk', q_r, k_blk).astype(jnp.float32) + bias.astype(jnp.float32)
        s = jnp.where(mask, s, -jnp.inf)
        p = jax.nn.softmax(s, axis=-1).astype(v.dtype)
        o = jnp.einsum('bhcqk,bckhd->bcqhd', p, v_blk)
        return o.reshape(b, GRID_W, d)

    out = lax.map(row_block, jnp.arange(rows))
    out = jnp.moveaxis(out, 0, 1).reshape(b, l, d)
    return jnp.einsum('bld,de->ble', out, w_o)


def _delta_rule_scan(r, w, k, v, a, bb, reverse):
    def step(S, inp):
        r_t, w_t, k_t, v_t, a_t, b_t = inp
        sa = jnp.einsum('bhvk,bhk->bhv', S, a_t)
        S = S * w_t[:, :, None, :] + sa[..., None] * b_t[:, :, None, :] + v_t[..., None] * k_t[:, :, None, :]
        return S, jnp.einsum('bhvk,bhk->bhv', S, r_t)
    s0 = jnp.zeros(r.shape[1:3] + (RWKV_HEAD_DIM, RWKV_HEAD_DIM), jnp.float32)
    _, y = lax.scan(step, s0, (r, w, k, v, a, bb), reverse=reverse)
    return jnp.moveaxis(y, 0, 1)


def rwkv7_bidirectional(x, mu, w_rkv, w0, w1, w2, a0, a1, a2, g1, g2, k_k, k_a, r_k, lnx_g, lnx_b, w_o):
    b, l, d = x.shape
    f32 = jnp.float32
    xp = jnp.pad(x, ((0, 0), (1, 1), (0, 0)))
    xx = 0.5 * (xp[:, :-2] + xp[:, 2:]) - x
    mix = lambda i: x + xx * mu[i]
    r, k, v = jnp.einsum('nbld,nde->nble', jnp.stack([mix(0), mix(2), mix(3)]), w_rkv)
    wl = w0[:, None, None, :] + jnp.einsum('zblr,zrd->zbld', jnp.tanh(jnp.einsum('bld,zdr->zblr', mix(1), w1)), w2)
    decay = jnp.exp(-jnp.exp(-jax.nn.softplus(-wl.astype(f32)) - 0.5))
    a = jax.nn.sigmoid((a0[:, None, None, :] + jnp.einsum('zblr,zrd->zbld', jnp.einsum('bld,zdr->zblr', mix(4), a1), a2)).astype(f32))
    g = jnp.einsum('blr,rd->bld', jax.nn.sigmoid(jnp.einsum('bld,dr->blr', mix(5), g1)), g2)
    heads = lambda t: t.reshape(t.shape[:-1] + (RWKV_HEADS, RWKV_HEAD_DIM))
    rf, kf, vf = heads(r.astype(f32)), heads(k.astype(f32)), heads(v.astype(f32))
    kk = heads(k.astype(f32) * k_k.astype(f32))
    kk = kk / jnp.maximum(jnp.sqrt(jnp.sum(kk * kk, axis=-1, keepdims=True)), 1e-12)
    a_h = heads(a)
    k_dir = kf[None] * (1.0 + (a_h - 1.0) * heads(k_a.astype(f32)))
    bb = kk[None] * a_h
    decay = heads(decay)
    tm = lambda t: jnp.moveaxis(t, 1, 0)
    r_t, v_t, na_t = tm(rf), tm(vf), tm(-kk)
    y_f = _delta_rule_scan(r_t, tm(decay[0]), tm(k_dir[0]), v_t, na_t, tm(bb[0]), reverse=False)
    y_b = _delta_rule_scan(r_t, tm(decay[1]), tm(k_dir[1]), v_t, na_t, tm(bb[1]), reverse=True)
    y = y_f + y_b
    yc = y - jnp.mean(y, axis=-1, keepdims=True)
    yn = (yc * lax.rsqrt(jnp.mean(yc * yc, axis=-1, keepdims=True) + GN_EPS)).reshape(b, l, d)
    yn = yn * lnx_g.astype(f32) + lnx_b.astype(f32)
    bonus = (jnp.sum(rf * (k_dir[0] + k_dir[1]) * r_k.astype(f32), axis=-1, keepdims=True) * vf).reshape(b, l, d)
    out = ((yn + bonus) * g.astype(f32)).astype(x.dtype)
    return jnp.einsum('bld,de->ble', out, w_o)


def swiglu(x, w_gate, w_up, w_down):
    h = jax.nn.silu(jnp.einsum('bld,df->blf', x, w_gate)) * jnp.einsum('bld,df->blf', x, w_up)
    return jnp.einsum('blf,fd->bld', h, w_down)


def moe_swiglu(x, w_router, b_router, w_gate, w_up, w_down):
    logits = jnp.einsum('bld,de->ble', x, w_router).astype(jnp.float32) + b_router.astype(jnp.float32)
    top_v, top_i = lax.top_k(logits, TOP_K)
    gates = jax.nn.softmax(top_v, axis=-1)
    combine = jnp.sum(jax.nn.one_hot(top_i, N_EXPERTS, dtype=jnp.float32) * gates[..., None], axis=-2).astype(x.dtype)
    out = jnp.zeros_like(x)
    for e in range(N_EXPERTS):
        out = out + combine[..., e:e + 1] * swiglu(x, w_gate[e], w_up[e], w_down[e])
    return out


def encoder_trunk(x, na_w_qkv, na_rpb, na_w_o, ffn_w_gate, ffn_w_up, ffn_w_down,
                  rwkv_mu, rwkv_w_rkv, rwkv_w0, rwkv_w1, rwkv_w2, rwkv_a0, rwkv_a1, rwkv_a2,
                  rwkv_g1, rwkv_g2, rwkv_k_k, rwkv_k_a, rwkv_r_k, rwkv_lnx_g, rwkv_lnx_b, rwkv_w_o,
                  moe_w_router, moe_b_router, moe_w_gate, moe_w_up, moe_w_down,
                  ln_mix_g, ln_mix_b, ln_ffn_g, ln_ffn_b):
    for i in range(DEPTH):
        j = i // 2
        if i % 2 == 0:
            h = neighbourhood_attention(x, na_w_qkv[j], na_rpb[j], na_w_o[j])
        else:
            h = rwkv7_bidirectional(x, rwkv_mu[j], rwkv_w_rkv[j], rwkv_w0[j], rwkv_w1[j], rwkv_w2[j],
                                    rwkv_a0[j], rwkv_a1[j], rwkv_a2[j], rwkv_g1[j], rwkv_g2[j],
                                    rwkv_k_k[j], rwkv_k_a[j], rwkv_r_k[j], rwkv_lnx_g[j], rwkv_lnx_b[j], rwkv_w_o[j])
        x = layer_norm(ALPHA * x + h, ln_mix_g[i], ln_mix_b[i])
        if i % 2 == 0:
            f = swiglu(x, ffn_w_gate[j], ffn_w_up[j], ffn_w_down[j])
        else:
            f = moe_swiglu(x, moe_w_router[j], moe_b_router[j], moe_w_gate[j], moe_w_up[j], moe_w_down[j])
        x = layer_norm(ALPHA * x + f, ln_ffn_g[i], ln_ffn_b[i])
    return x


def setup_inputs(seed: int = 0) -> dict:
    key = jax.random.key(seed)
    ks = iter(jax.random.split(key, 64))
    nrm = lambda shape, scale: jax.random.normal(next(ks), shape, jnp.float32) * scale
    D = D_MODEL
    s = D ** -0.5
    inp = {}
    inp['x_prompt'] = nrm((BATCH, SEQ, D), 1.0)
    inp['x_sample'] = nrm((DEC_BATCH, DEC_SEQ, D), 1.0)
    inp['na_w_qkv'] = jnp.concatenate([nrm((N_EVEN, D, D), s), nrm((N_EVEN, D, D), s), nrm((N_EVEN, D, D), s * BETA)], axis=-1)
    inp['na_rpb'] = nrm((N_EVEN, NA_HEADS, 2 * NA_ROWS - 1, 2 * NA_COLS - 1), 0.02)
    inp['na_w_o'] = nrm((N_EVEN, D, D), s * BETA)
    inp['ffn_w_gate'] = nrm((N_EVEN, D, D_FF), s * BETA)
    inp['ffn_w_up'] = nrm((N_EVEN, D, D_FF), s * BETA)
    inp['ffn_w_down'] = nrm((N_EVEN, D_FF, D), D_FF ** -0.5 * BETA)
    inp['rwkv_mu'] = jax.random.uniform(next(ks), (N_ODD, 6, D), jnp.float32)
    inp['rwkv_w_rkv'] = jnp.stack([nrm((N_ODD, D, D), s), nrm((N_ODD, D, D), s), nrm((N_ODD, D, D), s * BETA)], axis=1)
    inp['rwkv_w0'] = jnp.linspace(-6.0, -1.0, D, dtype=jnp.float32) + 0.5 + nrm((N_ODD, 2, D), 0.1)
    inp['rwkv_w1'] = nrm((N_ODD, 2, D, DECAY_LORA), s)
    inp['rwkv_w2'] = nrm((N_ODD, 2, DECAY_LORA, D), 0.1 * DECAY_LORA ** -0.5)
    inp['rwkv_a0'] = nrm((N_ODD, 2, D), 0.1)
    inp['rwkv_a1'] = nrm((N_ODD, 2, D, ICLR_LORA), s)
    inp['rwkv_a2'] = nrm((N_ODD, 2, ICLR_LORA, D), 0.1 * ICLR_LORA ** -0.5)
    inp['rwkv_g1'] = nrm((N_ODD, D, GATE_LORA), s)
    inp['rwkv_g2'] = nrm((N_ODD, GATE_LORA, D), GATE_LORA ** -0.5)
    inp['rwkv_k_k'] = 0.85 + nrm((N_ODD, D), 0.02)
    inp['rwkv_k_a'] = 1.0 + nrm((N_ODD, D), 0.02)
    inp['rwkv_r_k'] = nrm((N_ODD, RWKV_HEADS, RWKV_HEAD_DIM), 0.1)
    inp['rwkv_lnx_g'] = 1.0 + nrm((N_ODD, D), 0.02)
    inp['rwkv_lnx_b'] = nrm((N_ODD, D), 0.02)
    inp['rwkv_w_o'] = nrm((N_ODD, D, D), s * BETA)
    inp['moe_w_router'] = nrm((N_ODD, D, N_EXPERTS), s)
    inp['moe_b_router'] = nrm((N_ODD, N_EXPERTS), 0.01)
    inp['moe_w_gate'] = nrm((N_ODD, N_EXPERTS, D, D_EXPERT), s * BETA)
    inp['moe_w_up'] = nrm((N_ODD, N_EXPERTS, D, D_EXPERT), s * BETA)
    inp['moe_w_down'] = nrm((N_ODD, N_EXPERTS, D_EXPERT, D), D_EXPERT ** -0.5 * BETA)
    inp['ln_mix_g'] = 1.0 + nrm((DEPTH, D), 0.02)
    inp['ln_mix_b'] = nrm((DEPTH, D), 0.02)
    inp['ln_ffn_g'] = 1.0 + nrm((DEPTH, D), 0.02)
    inp['ln_ffn_b'] = nrm((DEPTH, D), 0.02)
    return inp


def reference(x_prompt, x_sample, na_w_qkv, na_rpb, na_w_o, ffn_w_gate, ffn_w_up, ffn_w_down,
              rwkv_mu, rwkv_w_rkv, rwkv_w0, rwkv_w1, rwkv_w2, rwkv_a0, rwkv_a1, rwkv_a2,
              rwkv_g1, rwkv_g2, rwkv_k_k, rwkv_k_a, rwkv_r_k, rwkv_lnx_g, rwkv_lnx_b, rwkv_w_o,
              moe_w_router, moe_b_router, moe_w_gate, moe_w_up, moe_w_down,
              ln_mix_g, ln_mix_b, ln_ffn_g, ln_ffn_b):
    weights = (na_w_qkv, na_rpb, na_w_o, ffn_w_gate, ffn_w_up, ffn_w_down,
               rwkv_mu, rwkv_w_rkv, rwkv_w0, rwkv_w1, rwkv_w2, rwkv_a0, rwkv_a1, rwkv_a2,
               rwkv_g1, rwkv_g2, rwkv_k_k, rwkv_k_a, rwkv_r_k, rwkv_lnx_g, rwkv_lnx_b, rwkv_w_o,
               moe_w_router, moe_b_router, moe_w_gate, moe_w_up, moe_w_down,
               ln_mix_g, ln_mix_b, ln_ffn_g, ln_ffn_b)
    y_prompt = encoder_trunk(x_prompt, *weights)
    y_sample = encoder_trunk(x_sample, *weights)
    return (y_prompt, y_sample)
```

```python
import contextlib
import os
import numpy as np
import concourse.bass as bass
import concourse.mybir as mybir
from concourse.bass_utils import run_bass_kernel_spmd

F32 = mybir.dt.float32
BF16 = mybir.dt.bfloat16
I32 = mybir.dt.int32
AF = mybir.ActivationFunctionType
ALU = mybir.AluOpType
AX = mybir.AxisListType

D = 1024
ALPHA = 4.0 ** 0.25
LN_EPS = 1e-5
GN_EPS = 64e-5
DFF = 2816
NE = 8
DEX = 3584
ADEC = float(np.exp(-0.5))
NEG = -30000.0
ENGS = ('pe', 'act', 'dve', 'pool', 'sp')
STG = int(os.environ.get('SCAN_STG', '99'))
EPOCH = 8000
DEPOCH = 900


class Op:
    __slots__ = ('eng', 'fn', 'waits', 'is_dma', 'idx', 'has_dep', 'sig', 'dsem', 'dval', 'prewait')

    def __init__(self, eng, fn, is_dma):
        self.eng = eng
        self.fn = fn
        self.waits = []
        self.is_dma = is_dma
        self.has_dep = False
        self.sig = None
        self.dsem = None
        self.dval = None
        self.prewait = None


class Sched:
    def __init__(self, nc, same_engine_sync=(os.environ.get('SES', '1') == '1'), dma_pool=8):
        self.nc = nc
        self.ops = {e: [] for e in ENGS}
        self.last_w = {}
        self.readers = {}
        self.seen = {e: {} for e in ENGS}
        self.seen_dma = {e: set() for e in ENGS}
        self.same_engine_sync = same_engine_sync
        self.dma_pool = dma_pool
        self.dma_count = {e: 0 for e in ENGS}
        self.dma_hist = {e: [] for e in ENGS}
        self.pending = {}
        self.arrive_fn = {}

    def _dep(self, op, d):
        if d is None or d is op:
            return
        e = op.eng
        if d.is_dma:
            if id(d) in self.seen_dma[e]:
                return
            self.seen_dma[e].add(id(d))
            op.waits.append(d)
            d.has_dep = True
            return
        if d.eng == e and not op.is_dma:
            if e == 'pe' or not self.same_engine_sync:
                return
        if self.seen[e].get(d.eng, -1) >= d.idx:
            return
        self.seen[e][d.eng] = d.idx
        op.waits.append(d)
        d.has_dep = True

    def add(self, eng, fn, reads=(), writes=(), is_dma=False):
        op = Op(eng, fn, is_dma)
        op.idx = len(self.ops[eng])
        pr_ = [b for b in reads if b[:2] == 'ps' and b[2:].isdigit()]
        if pr_:
            writes = list(writes) + pr_
        for a in self.pending.pop(eng, ()):
            self._dep(op, a)
        for b in reads:
            self._dep(op, self.last_w.get(b))
        for b in writes:
            self._dep(op, self.last_w.get(b))
            for r in self.readers.get(b, ()):
                self._dep(op, r)
        for b in reads:
            self.readers.setdefault(b, []).append(op)
        for b in writes:
            self.last_w[b] = op
            self.readers[b] = []
        if is_dma:
            k = self.dma_count[eng]
            self.dma_count[eng] += 1
            slot = k % self.dma_pool
            use = k // self.dma_pool
            op.dsem = (eng, slot, use // DEPOCH)
            op.dval = 16 * (use % DEPOCH + 1)
            if k >= self.dma_pool:
                prev = self.dma_hist[eng][k - self.dma_pool]
                op.prewait = prev
                self.seen_dma[eng].add(id(prev))
            self.dma_hist[eng].append(op)
        self.ops[eng].append(op)
        return op

    def dma(self, eng, out, in_, reads=(), writes=()):
        return self.add(eng, lambda e: e.dma_start(out=out, in_=in_), reads, writes, is_dma=True)

    def barrier(self):
        arr = []
        for e in ENGS:
            if e == 'pe':
                if self.ops['pe']:
                    arr.append(self.ops['pe'][-1])
                continue
            if not self.ops[e]:
                continue
            op = self.add(e, self.arrive_fn[e], writes=['_arr_' + e], is_dma=(e == 'sp'))
            n = self.dma_count[e]
            for k in range(max(0, n - self.dma_pool - 1), n):
                self._dep(op, self.dma_hist[e][k])
            arr.append(op)
        self.last_w.clear()
        self.readers.clear()
        self.pending = {e: list(arr) for e in ENGS}

    def emit(self):
        nc = self.nc
        nsig = {}
        for e in ENGS:
            c = 0
            for op in self.ops[e]:
                if (not op.is_dma) and op.has_dep:
                    op.sig = c
                    c += 1
            nsig[e] = c
        with contextlib.ExitStack() as st:
            csem = {}
            for e in ENGS:
                n = (nsig[e] + EPOCH - 1) // EPOCH
                csem[e] = [st.enter_context(nc.semaphore(f"c_{e}_{i}")) for i in range(n)]
            dsem = {}
            for e in ENGS:
                for op in self.ops[e]:
                    if op.is_dma and op.dsem not in dsem:
                        dsem[op.dsem] = st.enter_context(nc.semaphore("d_%s_%d_%d" % op.dsem))
            block = st.enter_context(nc.Block())

            def wait_for(eng_obj, d):
                if d.is_dma:
                    eng_obj.wait_ge(dsem[d.dsem], d.dval)
                else:
                    eng_obj.wait_ge(csem[d.eng][d.sig // EPOCH], d.sig % EPOCH + 1)

            def run(ename, eng_obj):
                for op in self.ops[ename]:
                    if op.prewait is not None:
                        wait_for(eng_obj, op.prewait)
                    for d in op.waits:
                        wait_for(eng_obj, d)
                    ins = op.fn(eng_obj)
                    if op.is_dma:
                        ins.then_inc(dsem[op.dsem], 16)
                    elif op.has_dep:
                        ins.then_inc(csem[ename][op.sig // EPOCH], 1)
                n = self.dma_count[ename]
                for k in range(max(0, n - self.dma_pool), n):
                    wait_for(eng_obj, self.dma_hist[ename][k])

            if self.ops['sp']:
                @block.sync
                def _(e):
                    run('sp', e)
            if self.ops['act']:
                @block.scalar
                def _(e):
                    run('act', e)
            if self.ops['dve']:
                @block.vector
                def _(e):
                    run('dve', e)
            if self.ops['pool']:
                @block.gpsimd
                def _(e):
                    run('pool', e)
            if self.ops['pe']:
                @block.tensor
                def _(e):
                    run('pe', e)
        return {e: len(self.ops[e]) for e in ENGS}


def _na_dr(ctype, rows, kr, r):
    if ctype == 'P':
        lo, hi = 0, rows
    else:
        lo, hi = (0, rows // 2) if r < rows // 2 else (rows // 2, rows)
    if not (lo <= kr < hi):
        return -1
    rs = min(max(r - 4, lo), hi - 8)
    if rs <= kr < rs + 8:
        return kr - r + 7
    return -1


_PLAN_CACHE = {}


def na_plan(rows):
    if rows in _PLAN_CACHE:
        return _PLAN_CACHE[rows]
    slots = []
    work = []

    def find_or_add(seq):
        n = len(seq)
        for s in range(len(slots) - n + 1):
            if slots[s:s + n] == seq:
                return s
        for ov in range(min(n, len(slots)), 0, -1):
            if slots[len(slots) - ov:] == seq[:ov]:
                s = len(slots) - ov
                slots.extend(seq[ov:])
                return s
        s = len(slots)
        slots.extend(seq)
        return s

    for g in range(rows // 8):
        lst = []
        for j in range(rows // 2):
            rr = []
            ds = []
            for r in range(8 * g, 8 * g + 8):
                d = (_na_dr('P', rows, 2 * j, r), _na_dr('S', rows, 2 * j, r),
                     _na_dr('P', rows, 2 * j + 1, r), _na_dr('S', rows, 2 * j + 1, r))
                if max(d) >= 0:
                    rr.append(r)
                    ds.append(d)
            if not rr:
                continue
            assert rr == list(range(rr[0], rr[-1] + 1))
            lst.append((j, rr[0], rr[-1], find_or_add(ds)))
        work.append(lst)
    _PLAN_CACHE[rows] = (work, slots)
    return work, slots


def na_table(rpb, ctype, slots):
    H = rpb.shape[0]
    kc = np.arange(64)[:, None]
    qc = np.arange(64)[None, :]
    qs = np.clip(qc - 8, 0, 48)
    colvalid = (kc >= qs) & (kc < qs + 16)
    dc = np.clip(kc - qc + 15, 0, 30)
    tab = np.full((H, 128, len(slots), 64), NEG, np.float32)
    ci = 0 if ctype == 'P' else 1
    for s, d in enumerate(slots):
        for half in range(2):
            dr = d[2 * half + ci]
            if dr < 0:
                continue
            vals = rpb[:, dr, :][:, dc]
            tab[:, half * 64:(half + 1) * 64, s, :] = np.where(colvalid[None], vals, np.float32(NEG))
    return tab


PNAMES = ['ln_mix_g0', 'ln_mix_b0', 'ln_ffn_g0', 'ln_ffn_b0', 'ln_mix_g1', 'ln_mix_b1',
          'mu0', 'mu1', 'mu2', 'mu3', 'mu4', 'mu5', 'w0_0', 'w0_1', 'a0_0', 'a0_1',
          'k_k', 'k_a', 'r_k', 'lnx_g', 'lnx_b']
PCOL = {n: 8 * i for i, n in enumerate(PNAMES)}
NPCOL = 8 * len(PNAMES)

C_I, C_MUS, C_MUI, C_MLS, C_MLI, C_ONE, C_BON, C_EB = 0, 128, 256, 384, 512, 640, 768, 896
NCCOL = 904


def make_consts(cap1):
    c = np.zeros((128, NCCOL), np.float32)
    s = np.arange(128)[:, None]
    t = np.arange(128)[None, :]
    c[:, C_I:C_I + 128] = (s == t)
    c[:, C_MUS:C_MUS + 128] = (s < t)
    c[:, C_MUI:C_MUI + 128] = (s <= t)
    c[:, C_MLS:C_MLS + 128] = (s > t)
    c[:, C_MLI:C_MLI + 128] = (s >= t)
    c[:, C_ONE:C_ONE + 128] = 1.0
    c[:, C_BON:C_BON + 128] = ((s // 64) == (t // 64))
    c[:, C_EB:C_EB + 8] = np.arange(8)[None, :] * cap1
    return c


def moe_cap(T):
    c = T // 4 + T // 16
    return (c + 127) // 128 * 128


class Builder:
    def __init__(self, T, dbg=(), stop_after=None):
        self.T = T
        self.ROWS = T // 64
        self.NT = T // 512
        self.dbg = set(dbg)
        self.stop_after = stop_after
        self.CAP = moe_cap(T)
        self.CAP1 = self.CAP + 128
        self.work, self.slots = na_plan(self.ROWS)
        self.NS = len(self.slots)
        self.nc = bass.Bass("TRN2", target_bir_lowering=False)
        self.d = {}
        self.uid = 0

    def din(self, name, shape, dt=F32):
        self.d[name] = self.nc.dram_tensor(name, list(shape), dt, kind="ExternalInput").ap()

    def dscr(self, name, shape, dt):
        kind = "ExternalOutput" if name in self.dbg else "Internal"
        self.d[name] = self.nc.dram_tensor(name, list(shape), dt, kind=kind).ap()

    def declare(self):
        T = self.T
        self.din('xT', [D, T])
        self.din('natab', [32, 128, self.NS * 64])
        self.din('params', [128, NPCOL])
        self.din('consts', [128, NCCOL])
        self.din('flags', [128, 2])
        self.din('lnf', [2, 128, D])
        self.din('brt', [128, 8])
        self.din('na_w_qkv', [D, 3 * D])
        self.din('na_w_o', [D, D])
        self.din('ffn_w_gate', [D, DFF])
        self.din('ffn_w_up', [D, DFF])
        self.din('ffn_w_down', [DFF, D])
        self.din('rwkv_w_rkv', [3, D, D])
        self.din('rwkv_w1', [2, D, 64])
        self.din('rwkv_w2', [2, 64, D])
        self.din('rwkv_a1', [2, D, 64])
        self.din('rwkv_a2', [2, 64, D])
        self.din('rwkv_g1', [D, 160])
        self.din('rwkv_g2', [160, D])
        self.din('rwkv_w_o', [D, D])
        self.din('moe_w_router', [D, NE])
        self.din('moe_w_gate', [NE, D, DEX])
        self.din('moe_w_up', [NE, D, DEX])
        self.din('moe_w_down', [NE, DEX, D])
        self.d['y'] = self.nc.dram_tensor('y', [T, D], F32, kind="ExternalOutput").ap()
        for n in ('qT', 'kT', 'oT', 'r16', 'kk16', 'k0', 'k1', 'b0', 'b1', 'gate16', 'bonus16', 'out16'):
            self.dscr(n, [D, T], BF16)
        for n in ('xmid', 'x1', 'sg0', 'sg1', 'x2'):
            self.dscr(n, [D, T], F32)
        self.dscr('vaug', [T, 32 * 64], BF16)
        self.dscr('vtok', [T, D], BF16)
        self.dscr('x2tok', [T, D], F32)
        self.dscr('Xg', [NE * self.CAP1, D], BF16)
        self.dscr('Yg', [NE * self.CAP1, D], F32)
        self.dscr('bar', [1, 16], F32)
        self.dscr('yF', [D, T], F32)
        self.dscr('yB', [D, T], F32)

    def alloc(self, shape, dt, parts=128):
        n = int(np.prod(shape))
        sz = 2 if dt == BF16 else 4
        words = (n * sz + 3) // 4
        words = (words + 15) // 16 * 16
        off = self.off
        self.off += words
        assert self.off <= self.nwords, ("SBUF arena overflow", self.off, self.nwords)
        v = self.arena[0:parts, off:off + words]
        if dt != F32:
            v = v.bitcast(dt)
        v = v[:, 0:n]
        if len(shape) == 2:
            v = v.rearrange("p (a b) -> p a b", a=shape[0])
        elif len(shape) == 3:
            v = v.rearrange("p (a b c) -> p a b c", a=shape[0], b=shape[1])
        return v

    def key(self, base):
        self.uid += 1
        return f"{base}#{self.uid}"

    def phase_end(self):
        self.S.barrier()
        self.off = self.mark

    def mm(self, out, lhsT, rhs, start=True, stop=True, r=(), w=()):
        self.S.add('pe', lambda e: e.matmul(out, lhsT=lhsT, rhs=rhs, start=start, stop=stop), r, w)

    def act(self, out, in_, func, r, w, bias=None, scale=None, accum=None):
        kw = {}
        if bias is not None:
            kw['bias'] = bias
        if scale is not None:
            kw['scale'] = scale
        if accum is not None:
            kw['accum_out'] = accum
        self.S.add('act', lambda e: e.activation(out=out, in_=in_, func=func, **kw), r, w)

    def tt(self, out, in0, in1, op, r, w, eng='dve'):
        self.S.add(eng, lambda e: e.tensor_tensor(out=out, in0=in0, in1=in1, op=op), r, w)

    def ts(self, out, in0, s1, op0, r, w, s2=None, op1=None, eng='dve'):
        if op1 is None:
            self.S.add(eng, lambda e: e.tensor_scalar(out=out, in0=in0, scalar1=s1, scalar2=None, op0=op0), r, w)
        else:
            self.S.add(eng, lambda e: e.tensor_scalar(out=out, in0=in0, scalar1=s1, scalar2=s2, op0=op0, op1=op1), r, w)

    def stt(self, out, in0, scalar, in1, op0, op1, r, w):
        self.S.add('dve', lambda e: e.scalar_tensor_tensor(out=out, in0=in0, scalar=scalar, in1=in1, op0=op0, op1=op1), r, w)

    def cp(self, out, in_, r, w, eng='dve'):
        self.S.add(eng, lambda e: e.tensor_copy(out=out, in_=in_), r, w)

    def memset(self, ap, val, w, eng='pool'):
        self.S.add(eng, lambda e: e.memset(ap, val), (), w)

    def dma(self, q, out, in_, r=(), w=()):
        self.S.dma(q, out, in_, r, w)

    def loadw(self, dst, src, w, r=()):
        n = dst.shape[-1]
        step = 2048
        for a in range(0, n, step):
            b = min(n, a + step)
            if len(dst.shape) == 3:
                self.dma('pool', dst[:, :, a:b], src[:, :, a:b], r, w)
            else:
                self.dma('pool', dst[:, a:b], src[:, a:b], r, w)

    def evac(self, i, out, in_, r, w):
        if i % 2 == 0:
            self.act(out, in_, AF.Copy, r, w)
        else:
            self.cp(out, in_, r, w)

    def build(self):
        nc = self.nc
        self.declare()
        with contextlib.ExitStack() as st:
            self.nwords = 47500
            self.arena_t = st.enter_context(nc.sbuf_tensor("arena", [128, self.nwords], F32))
            self.arena = self.arena_t
            self.off = 0
            self.ps = [st.enter_context(nc.psum_tensor(f"ps{i}", [128, 512], F32)) for i in range(8)]
            self.S = Sched(nc)
            self.psi = 0
            self.setup_consts()
            phases = [self.phase_qkv, self.phase_attn, self.phase_wo0, self.phase_ffn,
                      self.phase_rwkv_prep, self.phase_scan, self.phase_wo1, self.phase_moe]
            for i, ph in enumerate(phases):
                ph()
                self.phase_end()
                if self.stop_after is not None and i >= self.stop_after:
                    break
            self.counts = self.S.emit()
        return nc

    def setup_consts(self):
        S = self.S
        self.cst = self.alloc([NCCOL], F32)
        self.par = self.alloc([NPCOL], F32)
        self.flg = self.alloc([2], F32)
        self.idb = self.alloc([128], BF16)
        self.omka = self.alloc([8], F32)
        self.scr1 = self.alloc([16], F32)
        self.dma('sp', self.cst, self.d['consts'], (), ['cst'])
        self.dma('sp', self.par, self.d['params'], (), ['par'])
        self.dma('sp', self.flg, self.d['flags'], (), ['flg'])
        self.cp(self.idb, self.cst[:, C_I:C_I + 128], ['cst'], ['idb'])
        kcol = PCOL['k_a']
        self.ts(self.omka, self.par[:, kcol:kcol + 8], -1.0, ALU.mult, ['par'], ['omka'], s2=1.0, op1=ALU.add)
        self.memset(self.scr1, 0.0, ['scr1'], eng='pool')
        scr1 = self.scr1
        bar = self.d['bar']
        S.arrive_fn = {
            'sp': lambda e: e.dma_start(out=bar[0:1, 0:4], in_=bar[0:1, 8:12]),
            'act': lambda e: e.activation(out=scr1[0:1, 0:1], in_=scr1[0:1, 1:2], func=AF.Copy),
            'dve': lambda e: e.memset(scr1[0:1, 4:5], 0.0),
            'pool': lambda e: e.memset(scr1[0:1, 8:9], 0.0),
        }
        self.mark = self.off
        S.barrier()

    def pcol(self, name, c):
        o = PCOL[name] + c
        return self.par[:, o:o + 1]

    def ln_fm(self, z, zk, N, gname, bname, out32, o32k, tmp, out16=None, o16k=None):
        ones = self.cst[:, C_ONE:C_ONE + 128]
        psA, kA = self.ps[6], 'ps6'
        psB, kB = self.ps[7], 'ps7'
        sq = tmp['sq']
        for c in range(8):
            s = sq[c % 2]
            sk = tmp['sqk'][c % 2]
            self.act(s[:, 0:N], z[:, c, :], AF.Square, [zk[c]], [sk])
            self.mm(psA[:, 0:N], ones, z[:, c, :], c == 0, c == 7, [zk[c], 'cst'], [kA])
            self.mm(psB[:, 0:N], ones, s[:, 0:N], c == 0, c == 7, [sk, 'cst'], [kB])
        mean, rstd, m2 = tmp['mean'], tmp['rstd'], tmp['m2']
        mk, rk, m2k = tmp['meank'], tmp['rstdk'], tmp['m2k']
        self.act(mean[:, 0:N], psA[:, 0:N], AF.Copy, [kA], [mk], scale=1.0 / D)
        self.tt(m2[:, 0:N], mean[:, 0:N], mean[:, 0:N], ALU.mult, [mk], [m2k])
        self.stt(m2[:, 0:N], psB[:, 0:N], 1.0 / D, m2[:, 0:N], ALU.mult, ALU.subtract, [kB, m2k], [m2k])
        self.ts(m2[:, 0:N], m2[:, 0:N], LN_EPS, ALU.add, [m2k], [m2k])
        self.act(rstd[:, 0:N], m2[:, 0:N], AF.Sqrt, [m2k], [rk])
        self.S.add('dve', lambda e: e.reciprocal(out=rstd[:, 0:N], in_=rstd[:, 0:N]), [rk], [rk])
        for c in range(8):
            t = tmp['t'][c % 2]
            tk = tmp['tk'][c % 2]
            self.tt(t[:, 0:N], z[:, c, :], mean[:, 0:N], ALU.subtract, [zk[c], mk], [tk])
            self.tt(t[:, 0:N], t[:, 0:N], rstd[:, 0:N], ALU.mult, [tk, rk], [tk])
            self.act(out32[:, c, :], t[:, 0:N], AF.Identity, [tk, 'par'], [o32k[c]],
                     bias=self.pcol(bname, c), scale=self.pcol(gname, c))
            if out16 is not None:
                self.cp(out16[:, c, :], out32[:, c, :], [o32k[c]], [o16k[c]], eng='pool')

    def ln_tmp(self, N):
        t = {}
        t['sq'] = [self.alloc([N], F32) for _ in range(2)]
        t['sqk'] = [self.key('sq') for _ in range(2)]
        t['t'] = [self.alloc([N], F32) for _ in range(2)]
        t['tk'] = [self.key('lt') for _ in range(2)]
        for n in ('mean', 'rstd', 'm2'):
            t[n] = self.alloc([N], F32)
            t[n + 'k'] = self.key(n)
        return t

    def phase_qkv(self):
        T, NT = self.T, self.NT
        wq = self.alloc([8, 3 * D], BF16)
        self.loadw(wq, self.d['na_w_qkv'].rearrange("(c p) n -> p c n", p=128), ['wq'])
        xin = [self.alloc([8, 512], F32) for _ in range(2)]
        xb = [self.alloc([8, 512], BF16) for _ in range(2)]
        qk = [self.alloc([16, 512], BF16) for _ in range(2)]
        vt = [self.alloc([4, 32, 64], BF16) for _ in range(2)]
        vtk = [[f'vt{b}_{j}' for j in range(8)] for b in range(2)]
        for b in range(2):
            self.memset(vt[b][:, :, :, 32:64], 1.0, vtk[b])
        xTv = self.d['xT'].rearrange("(c p) t -> p c t", p=128)
        qTv = self.d['qT'].rearrange("(c p) t -> p c t", p=128)
        kTv = self.d['kT'].rearrange("(c p) t -> p c t", p=128)
        vav = self.d['vaug'].rearrange("(n s p) (h d) -> n p s h d", p=128, s=4, d=64)
        ev = 0
        for i in range(NT):
            b = i % 2
            sl = slice(i * 512, (i + 1) * 512)
            self.dma('sp', xin[b], xTv[:, :, sl], (), [f'xin{b}'])
            self.act(xb[b], xin[b], AF.Copy, [f'xin{b}'], [f'xb{b}'])
            for oc in range(16):
                pb = oc % 4
                for kc in range(8):
                    self.mm(self.ps[pb][:, :], wq[:, kc, oc * 128:(oc + 1) * 128], xb[b][:, kc, :], kc == 0, kc == 7,
                            ['wq', f'xb{b}'], [f'ps{pb}'])
                self.evac(ev, qk[b][:, oc, :], self.ps[pb][:, :], [f'ps{pb}'], [f'qk{b}_{oc}'])
                ev += 1
            self.dma('sp', qTv[:, :, sl], qk[b][:, 0:8, :], [f'qk{b}_{oc}' for oc in range(8)], ())
            self.dma('sp', kTv[:, :, sl], qk[b][:, 8:16, :], [f'qk{b}_{oc}' for oc in range(8, 16)], ())
            for stt_ in range(4):
                for half in range(2):
                    pb = 4 + (stt_ * 2 + half) % 2
                    for kc in range(8):
                        self.mm(self.ps[pb][:, :], xb[b][:, kc, stt_ * 128:(stt_ + 1) * 128],
                                wq[:, kc, 2 * D + half * 512:2 * D + (half + 1) * 512], kc == 0, kc == 7,
                                ['wq', f'xb{b}'], [f'ps{pb}'])
                    self.evac(ev, vt[b][:, stt_, half * 16:(half + 1) * 16, 0:32],
                              self.ps[pb][:, :].rearrange("p (h d) -> p h d", d=32),
                              [f'ps{pb}'], [vtk[b][stt_ * 2 + half]])
                    ev += 1
            self.dma('sp', vav[i], vt[b], vtk[b], ())

    def phase_attn(self):
        T = self.T
        NS = self.NS
        NKT = T // 128
        NB = 4
        LA = 2
        tabf = self.alloc([NS * 64], F32)
        Et = [self.alloc([NS * 64], BF16) for _ in range(2)]
        kh = [self.alloc([T], BF16, parts=32) for _ in range(2)]
        vh = [self.alloc([NKT, 64], BF16) for _ in range(2)]
        qg = [self.alloc([512], BF16, parts=32) for _ in range(3)]
        og = [self.alloc([512], BF16, parts=32) for _ in range(3)]
        rec = [self.alloc([512], F32, parts=64) for _ in range(2)]
        es = [self.alloc([512], BF16) for _ in range(NB)]
        pt = [self.alloc([512], BF16) for _ in range(NB)]
        vav = self.d['vaug'].rearrange("(n p) f -> p n f", p=128)
        scale = float(32 ** -0.5)
        NG = len(self.work)

        def head_loads(h):
            hb = h % 2
            self.dma('sp', tabf, self.d['natab'][h], (), ['tabf'])
            self.act(Et[hb], tabf, AF.Exp, ['tabf'], [f'Et{hb}'])
            self.dma('sp', kh[hb], self.d['kT'][h * 32:(h + 1) * 32, :], (), [f'kh{hb}'])
            nq = max(1, NKT // 16)
            for q4 in range(nq):
                a0, a1 = q4 * NKT // nq, (q4 + 1) * NKT // nq
                self.dma('sp', vh[hb][:, a0:a1, :], vav[:, a0:a1, h * 64:(h + 1) * 64], (), [f'vh{hb}'])

        def q_load(gi):
            h, g = divmod(gi, NG)
            q3 = gi % 3
            self.dma('sp', qg[q3], self.d['qT'][h * 32:(h + 1) * 32, g * 512:(g + 1) * 512], (), [f'qg{q3}'])

        items = []
        for h in range(32):
            for g, lst in enumerate(self.work):
                for idx, it_ in enumerate(lst):
                    items.append((h, g, idx, len(lst), it_))
        n = len(items)
        head_loads(0)
        q_load(0)
        for i in range(n + LA):
            if i < n:
                h, g, idx, ln_, (j, ra, rb, s0) = items[i]
                hb = h % 2
                gi = h * NG + g
                if idx == 0:
                    if gi + 1 < 32 * NG:
                        q_load(gi + 1)
                q3 = gi % 3
                nn = (rb - ra + 1) * 64
                qo = (ra - 8 * g) * 64
                b3 = i % NB
                pb = i % 6
                self.mm(self.ps[pb][:, 0:nn], kh[hb][:, j * 128:(j + 1) * 128], qg[q3][:, qo:qo + nn], True, True,
                        [f'kh{hb}', f'qg{q3}'], [f'ps{pb}'])
                self.act(es[b3][:, 0:nn], self.ps[pb][:, 0:nn], AF.Exp, [f'ps{pb}'], [f'es{b3}'], scale=scale)
                self.tt(pt[b3][:, 0:nn], es[b3][:, 0:nn], Et[hb][:, s0 * 64:s0 * 64 + nn], ALU.mult,
                        [f'es{b3}', f'Et{hb}'], [f'pt{b3}'])
            k = i - LA
            if k >= 0:
                h, g, idx, ln_, (j, ra, rb, s0) = items[k]
                if idx == 0 and g == 0 and h + 1 < 32:
                    head_loads(h + 1)
                hb = h % 2
                gi = h * NG + g
                q3 = gi % 3
                o2 = gi % 2
                nn = (rb - ra + 1) * 64
                qo = (ra - 8 * g) * 64
                b3 = k % NB
                pso, kso = self.ps[6 + o2], f'ps{6 + o2}'
                self.mm(pso[0:64, qo:qo + nn], vh[hb][:, j, :], pt[b3][:, 0:nn], idx == 0, idx == ln_ - 1,
                        [f'vh{hb}', f'pt{b3}'], [kso])
                if idx == ln_ - 1:
                    self.S.add('dve', lambda e, o=rec[o2], i_=pso: e.reciprocal(out=o[32:64, :], in_=i_[32:64, :]), [kso], [f'rec{o2}'])
                    self.tt(og[q3][0:32, :], pso[0:32, :], rec[o2][32:64, :], ALU.mult, [kso, f'rec{o2}'], [f'og{q3}'])
                    self.dma('sp', self.d['oT'][h * 32:(h + 1) * 32, g * 512:(g + 1) * 512], og[q3], [f'og{q3}'], ())

    def proj_res_ln(self, src16, wname, res32, gname, bname, dst32):
        T, NT = self.T, self.NT
        wo = self.alloc([8, D], BF16)
        self.loadw(wo, self.d[wname].rearrange("(c p) n -> p c n", p=128), ['wo'])
        ot = [self.alloc([8, 512], BF16) for _ in range(2)]
        xin = [self.alloc([8, 512], F32) for _ in range(2)]
        z = [self.alloc([8, 512], F32) for _ in range(2)]
        tmp = self.ln_tmp(512)
        sv = self.d[src16].rearrange("(c p) t -> p c t", p=128)
        rv = self.d[res32].rearrange("(c p) t -> p c t", p=128)
        dv = self.d[dst32].rearrange("(c p) t -> p c t", p=128)
        for i in range(NT):
            b = i % 2
            sl = slice(i * 512, (i + 1) * 512)
            self.dma('sp', ot[b], sv[:, :, sl], (), [f'ot{b}'])
            self.dma('sp', xin[b], rv[:, :, sl], (), [f'xin{b}'])
            zk = [f'z{b}_{c}' for c in range(8)]
            for oc in range(8):
                pb = oc % 4
                for kc in range(8):
                    self.mm(self.ps[pb][:, :], wo[:, kc, oc * 128:(oc + 1) * 128], ot[b][:, kc, :], kc == 0, kc == 7,
                            ['wo', f'ot{b}'], [f'ps{pb}'])
                self.stt(z[b][:, oc, :], xin[b][:, oc, :], ALPHA, self.ps[pb][:, :], ALU.mult, ALU.add,
                         [f'xin{b}', f'ps{pb}'], [zk[oc]])
            self.ln_fm(z[b], zk, 512, gname, bname, z[b], zk, tmp)
            self.dma('sp', dv[:, :, sl], z[b], zk, ())

    def phase_wo0(self):
        self.proj_res_ln('oT', 'na_w_o', 'xT', 'ln_mix_g0', 'ln_mix_b0', 'xmid')

    def phase_wo1(self):
        self.proj_res_ln('out16', 'rwkv_w_o', 'x1', 'ln_mix_g1', 'ln_mix_b1', 'x2')

    def phase_ffn(self):
        T, NT = self.T, self.NT
        NF = DFF // 128
        wg = self.alloc([8, DFF], BF16)
        wu = self.alloc([8, DFF], BF16)
        wd = self.alloc([NF, D], BF16)
        self.loadw(wg, self.d['ffn_w_gate'].rearrange("(c p) n -> p c n", p=128), ['wg'])
        self.loadw(wu, self.d['ffn_w_up'].rearrange("(c p) n -> p c n", p=128), ['wu'])
        self.loadw(wd, self.d['ffn_w_down'].rearrange("(c p) n -> p c n", p=128), ['wd'])
        z = self.alloc([8, 512], F32)
        xb = self.alloc([8, 512], BF16)
        groups = [(0, 6), (6, 12), (12, 17), (17, 22)]
        hh = self.alloc([6, 512], BF16)
        sgt = [self.alloc([512], BF16) for _ in range(2)]
        tmp = self.ln_tmp(512)
        sv = self.d['xmid'].rearrange("(c p) t -> p c t", p=128)
        dv = self.d['x1'].rearrange("(c p) t -> p c t", p=128)
        zk = [f'z_{c}' for c in range(8)]
        it = 0
        for i in range(NT):
            sl = slice(i * 512, (i + 1) * 512)
            self.dma('sp', z, sv[:, :, sl], (), zk)
            self.act(xb, z, AF.Copy, zk, ['xb'])
            self.ts(z, z, ALPHA, ALU.mult, zk, zk, eng='pool')
            for (f0, f1) in groups:
                for fc in range(f1 - f0):
                    f = f0 + fc
                    pg = it % 2
                    pu = 2 + it % 2
                    s2 = it % 2
                    it += 1
                    for kc in range(8):
                        self.mm(self.ps[pg][:, :], wg[:, kc, f * 128:(f + 1) * 128], xb[:, kc, :], kc == 0, kc == 7,
                                ['wg', 'xb'], [f'ps{pg}'])
                    for kc in range(8):
                        self.mm(self.ps[pu][:, :], wu[:, kc, f * 128:(f + 1) * 128], xb[:, kc, :], kc == 0, kc == 7,
                                ['wu', 'xb'], [f'ps{pu}'])
                    self.act(sgt[s2], self.ps[pg][:, :], AF.Silu, [f'ps{pg}'], [f'sgt{s2}'])
                    self.tt(hh[:, fc, :], sgt[s2], self.ps[pu][:, :], ALU.mult, [f'sgt{s2}', f'ps{pu}'], [f'hh{fc}'])
                for oc in range(8):
                    pd = 4 + oc % 2
                    for fc in range(f1 - f0):
                        f = f0 + fc
                        self.mm(self.ps[pd][:, :], wd[:, f, oc * 128:(oc + 1) * 128], hh[:, fc, :], fc == 0, fc == f1 - f0 - 1,
                                ['wd', f'hh{fc}'], [f'ps{pd}'])
                    self.tt(z[:, oc, :], z[:, oc, :], self.ps[pd][:, :], ALU.add, [zk[oc], f'ps{pd}'], [zk[oc]])
            self.ln_fm(z, zk, 512, 'ln_ffn_g0', 'ln_ffn_b0', z, zk, tmp)
            self.dma('sp', dv[:, :, sl], z, zk, ())

    def nb(self):
        i = self.psi
        self.psi = (i + 1) % 8
        return self.ps[i], f'ps{i}'

    def phase_rwkv_prep(self):
        T, NT = self.T, self.NT
        self.psi = 0
        wv3 = self.d['rwkv_w_rkv']
        wr = self.alloc([8, D], BF16)
        wk = self.alloc([8, D], BF16)
        wv = self.alloc([8, D], BF16)
        for i_, t_ in enumerate((wr, wk, wv)):
            self.loadw(t_, wv3[i_].rearrange("(c p) n -> p c n", p=128), [f'w3_{i_}'])
        w1 = [self.alloc([8, 64], BF16) for _ in range(2)]
        a1 = [self.alloc([8, 64], BF16) for _ in range(2)]
        w2 = [self.alloc([D], BF16, parts=64) for _ in range(2)]
        a2 = [self.alloc([D], BF16, parts=64) for _ in range(2)]
        for z in range(2):
            self.loadw(w1[z], self.d['rwkv_w1'][z].rearrange("(c p) n -> p c n", p=128), ['wl'])
            self.loadw(a1[z], self.d['rwkv_a1'][z].rearrange("(c p) n -> p c n", p=128), ['wl'])
            self.loadw(w2[z], self.d['rwkv_w2'][z], ['wl'])
            self.loadw(a2[z], self.d['rwkv_a2'][z], ['wl'])
        g1 = self.alloc([8, 160], BF16)
        g2a = self.alloc([D], BF16)
        g2b = self.alloc([D], BF16)
        self.loadw(g1, self.d['rwkv_g1'].rearrange("(c p) n -> p c n", p=128), ['wl'])
        self.loadw(g2a, self.d['rwkv_g2'][0:128, :], ['wl'])
        self.memset(g2b, 0.0, ['g2b'])
        self.loadw(g2b[0:32, :], self.d['rwkv_g2'][128:160, :], ['g2b'])
        bon = self.alloc([128], BF16)
        self.cp(bon, self.cst[:, C_BON:C_BON + 128], ['cst'], ['bon'])
        xh = self.alloc([8, 514], F32)
        xx = self.alloc([8, 512], F32)
        mix = [self.alloc([8, 512], BF16) for _ in range(4)]
        hidw = [self.alloc([512], BF16, parts=64) for _ in range(2)]
        hida = [self.alloc([512], BF16, parts=64) for _ in range(2)]
        gh = self.alloc([2, 512], BF16)
        self.memset(gh, 0.0, ['gh0', 'gh1'])
        NTMP = 10
        tmpf = [self.alloc([512], F32) for _ in range(NTMP)]
        NOB = 8
        ob = [self.alloc([512], BF16) for _ in range(NOB)]
        of = [self.alloc([512], F32) for _ in range(4)]
        rvb = [[self.alloc([512], BF16) for _ in range(2)] for _ in range(2)]
        cnt = {'t': 0, 'o': 0, 'f': 0, 'q': 0}

        def tmp():
            i = cnt['t'] % NTMP
            cnt['t'] += 1
            return tmpf[i], f'tmp{i}'

        def obuf():
            i = cnt['o'] % NOB
            cnt['o'] += 1
            return ob[i], f'ob{i}'

        def ofbuf():
            i = cnt['f'] % 4
            cnt['f'] += 1
            return of[i], f'of{i}'

        def outq():
            cnt['q'] += 1
            return 'sp'

        x1v = self.d['x1'].rearrange("(c p) t -> p c t", p=128)
        xk = [f'xh{c}' for c in range(8)]
        xxk = [f'xx{c}' for c in range(8)]

        def fm(name):
            return self.d[name].rearrange("(c p) t -> p c t", p=128)
        r16v, kk16v, gatev, bonv = fm('r16'), fm('kk16'), fm('gate16'), fm('bonus16')
        kzv = [fm('k0'), fm('k1')]
        bzv = [fm('b0'), fm('b1')]
        sgv = [fm('sg0'), fm('sg1')]
        vtv = self.d['vtok'].rearrange("(n p) f -> n p f", p=128)

        def make_mix(mi, dst, dk):
            for c in range(8):
                self.S.add('dve', lambda e, c=c: e.scalar_tensor_tensor(
                    out=dst[:, c, :], in0=xx[:, c, :], scalar=self.pcol(f'mu{mi}', c), in1=xh[:, c, 1:513],
                    op0=ALU.mult, op1=ALU.add), [xxk[c], xk[c], 'par'], [f'{dk}_{c}'])

        for i in range(NT):
            t0 = i * 512
            lo = 1 if i == 0 else 0
            hi = 513 if i == NT - 1 else 514
            if i == 0:
                self.memset(xh[:, :, 0:1], 0.0, xk)
            if i == NT - 1:
                self.memset(xh[:, :, 513:514], 0.0, xk)
            self.dma('sp', xh[:, :, lo:hi], x1v[:, :, t0 - 1 + lo:t0 - 1 + hi], (), xk)
            if i == NT // 2 - 1:
                self.ts(xh[:, :, 513:514], xh[:, :, 513:514], self.flg[:, 0:1], ALU.mult, xk + ['flg'], xk)
            if i == NT // 2:
                self.ts(xh[:, :, 0:1], xh[:, :, 0:1], self.flg[:, 0:1], ALU.mult, xk + ['flg'], xk)
            self.tt(xx, xh[:, :, 0:512], xh[:, :, 2:514], ALU.add, xk, xxk)
            self.stt(xx, xx, 0.5, xh[:, :, 1:513], ALU.mult, ALU.subtract, xxk + xk, xxk)
            m3k = [f'm3_{c}' for c in range(8)]
            make_mix(1, mix[3], 'm3')
            for z in range(2):
                ps, pk = self.nb()
                for kc in range(8):
                    self.mm(ps[0:64, :], w1[z][:, kc, :], mix[3][:, kc, :], kc == 0, kc == 7, ['wl', m3k[kc]], [pk])
                self.act(hidw[z], ps[0:64, :], AF.Tanh, [pk], [f'hidw{z}'])
            make_mix(4, mix[3], 'm3')
            for z in range(2):
                ps, pk = self.nb()
                for kc in range(8):
                    self.mm(ps[0:64, :], a1[z][:, kc, :], mix[3][:, kc, :], kc == 0, kc == 7, ['wl', m3k[kc]], [pk])
                self.cp(hida[z], ps[0:64, :], [pk], [f'hida{z}'])
            make_mix(5, mix[3], 'm3')
            ps, pk = self.nb()
            for kc in range(8):
                self.mm(ps[:, :], g1[:, kc, 0:128], mix[3][:, kc, :], kc == 0, kc == 7, ['wl', m3k[kc]], [pk])
            self.act(gh[:, 0, :], ps[:, :], AF.Sigmoid, [pk], ['gh0'])
            ps, pk = self.nb()
            for kc in range(8):
                self.mm(ps[0:32, :], g1[:, kc, 128:160], mix[3][:, kc, :], kc == 0, kc == 7, ['wl', m3k[kc]], [pk])
            self.act(gh[0:32, 1, :], ps[0:32, :], AF.Sigmoid, [pk], ['gh1'])
            make_mix(0, mix[0], 'm0')
            make_mix(2, mix[1], 'm1')
            make_mix(3, mix[2], 'm2')
            m0k = [f'm0_{c}' for c in range(8)]
            m1k = [f'm1_{c}' for c in range(8)]
            m2k = [f'm2_{c}' for c in range(8)]
            for st_ in range(4):
                for half in range(2):
                    ps, pk = self.nb()
                    for kc in range(8):
                        self.mm(ps[:, :], mix[2][:, kc, st_ * 128:(st_ + 1) * 128], wv[:, kc, half * 512:(half + 1) * 512],
                                kc == 0, kc == 7, ['w3_2', m2k[kc]], [pk])
                    o, ok = obuf()
                    self.evac(st_ + half, o, ps[:, :], [pk], [ok])
                    self.dma(outq(), vtv[i * 4 + st_][:, half * 512:(half + 1) * 512], o, [ok], ())
            for c in range(8):
                sl = slice(t0, t0 + 512)
                cs = slice(c * 128, (c + 1) * 128)
                psr, pkr = self.nb()
                for kc in range(8):
                    self.mm(psr[:, :], wr[:, kc, cs], mix[0][:, kc, :], kc == 0, kc == 7, ['w3_0', m0k[kc]], [pkr])
                rb, rbk = rvb[0][c % 2], f'rb{c % 2}'
                self.act(rb, psr[:, :], AF.Copy, [pkr], [rbk])
                self.dma(outq(), r16v[:, c, sl], rb, [rbk], ())
                psv, pkv = self.nb()
                for kc in range(8):
                    self.mm(psv[:, :], wv[:, kc, cs], mix[2][:, kc, :], kc == 0, kc == 7, ['w3_2', m2k[kc]], [pkv])
                vb, vbk = rvb[1][c % 2], f'vb{c % 2}'
                self.act(vb, psv[:, :], AF.Copy, [pkv], [vbk])
                psk, pkk = self.nb()
                for kc in range(8):
                    self.mm(psk[:, :], wk[:, kc, cs], mix[1][:, kc, :], kc == 0, kc == 7, ['w3_1', m1k[kc]], [pkk])
                kf, kfk = tmp()
                self.cp(kf, psk[:, :], [pkk], [kfk])
                al = []
                for z in range(2):
                    ps, pk = self.nb()
                    self.mm(ps[:, :], a2[z][:, cs], hida[z], True, True, ['wl', f'hida{z}'], [pk])
                    a_, ak = tmp()
                    self.act(a_, ps[:, :], AF.Sigmoid, [pk, 'par'], [ak], bias=self.pcol(f'a0_{z}', c))
                    al.append((a_, ak))
                    ps, pk = self.nb()
                    self.mm(ps[:, :], w2[z][:, cs], hidw[z], True, True, ['wl', f'hidw{z}'], [pk])
                    o, ok = ofbuf()
                    self.act(o, ps[:, :], AF.Sigmoid, [pk, 'par'], [ok], bias=self.pcol(f'w0_{z}', c))
                    self.dma(outq(), sgv[z][:, c, sl], o, [ok], ())
                ps, pk = self.nb()
                self.mm(ps[:, :], g2a[:, cs], gh[:, 0, :], True, False, ['wl', 'gh0'], [pk])
                self.mm(ps[:, :], g2b[:, cs], gh[:, 1, :], False, True, ['g2b', 'gh1'], [pk])
                o, ok = obuf()
                self.cp(o, ps[:, :], [pk], [ok])
                self.dma(outq(), gatev[:, c, sl], o, [ok], ())
                kq, kqk = tmp()
                self.ts(kq, kf, self.pcol('k_k', c), ALU.mult, [kfk, 'par'], [kqk])
                sq, sqk = obuf()
                self.act(sq, kq, AF.Square, [kqk], [sqk])
                ps, pk = self.nb()
                self.mm(ps[:, :], bon, sq, True, True, ['bon', sqk], [pk])
                nr, nrk = tmp()
                self.ts(nr, ps[:, :], 1e-24, ALU.max, [pk], [nrk])
                self.act(nr, nr, AF.Sqrt, [nrk], [nrk])
                self.S.add('dve', lambda e, nr=nr: e.reciprocal(out=nr, in_=nr), [nrk], [nrk])
                kkf, kkfk = tmp()
                self.tt(kkf, kq, nr, ALU.mult, [kqk, nrk], [kkfk])
                o, ok = obuf()
                self.cp(o, kkf, [kkfk], [ok], eng='pool')
                self.dma(outq(), kk16v[:, c, sl], o, [ok], ())
                tz = []
                for z in range(2):
                    a_, ak = al[z]
                    o, ok = obuf()
                    self.tt(o, kkf, a_, ALU.mult, [kkfk, ak], [ok])
                    self.dma(outq(), bzv[z][:, c, sl], o, [ok], ())
                    t_, tk_ = tmp()
                    self.ts(t_, a_, self.pcol('k_a', c), ALU.mult, [ak, 'par', 'omka'], [tk_],
                            s2=self.omka[:, c:c + 1], op1=ALU.add)
                    o, ok = obuf()
                    self.tt(o, t_, kf, ALU.mult, [tk_, kfk], [ok])
                    self.dma(outq(), kzv[z][:, c, sl], o, [ok], ())
                    tz.append((t_, tk_))
                tsum, tsk = tmp()
                self.tt(tsum, tz[0][0], tz[1][0], ALU.add, [tz[0][1], tz[1][1]], [tsk])
                self.tt(tsum, tsum, kf, ALU.mult, [tsk, kfk], [tsk])
                pr, prk = obuf()
                self.stt(pr, tsum, self.pcol('r_k', c), rb, ALU.mult, ALU.mult, [tsk, rbk, 'par'], [prk])
                ps, pk = self.nb()
                self.mm(ps[:, :], bon, pr, True, True, ['bon', prk], [pk])
                o, ok = obuf()
                self.tt(o, ps[:, :], vb, ALU.mult, [pk, vbk], [ok])
                self.dma(outq(), bonv[:, c, sl], o, [ok], ())

    def phase_scan(self):
        T = self.T
        NC = T // 128
        NCB = int(os.environ.get('SCAN_NCB', '4'))
        NQ = int(os.environ.get('SCAN_NQ', '2'))
        DB = int(os.environ.get('SCAN_DB', '1'))
        self.psi = 0
        mAB = self.alloc([4, 256], BF16)
        mC = self.alloc([4, 128], BF16)
        id4 = self.alloc([4, 128], BF16)
        cst = self.cst
        for q in range(4):
            z = q // 2
            self.cp(mAB[:, q, 0:128], cst[:, (C_MUS if z == 0 else C_MLS):(C_MUS if z == 0 else C_MLS) + 128], ['cst'], ['mAB'])
            self.cp(mAB[:, q, 128:256], cst[:, (C_MUI if z == 0 else C_MLI):(C_MUI if z == 0 else C_MLI) + 128], ['cst'], ['mAB'])
            self.cp(mC[:, q, :], cst[:, (C_MLS if z == 0 else C_MUS):(C_MLS if z == 0 else C_MUS) + 128], ['cst'], ['mC'])
            self.cp(id4[:, q, :], cst[:, C_I:C_I + 128], ['cst'], ['id4'])
        pm = self.alloc([2], F32)
        nm = self.alloc([2], F32)
        self.memset(pm, 0.0, ['pm'])
        self.memset(nm, 0.0, ['nm'])
        for hh in range(2):
            self.memset(pm[64 * hh:64 * hh + 64, hh:hh + 1], 1.0, ['pm'])
            self.memset(nm[64 * hh:64 * hh + 64, hh:hh + 1], -1.0, ['nm'])
        ones = self.alloc([512], F32)
        self.memset(ones, 1.0, ['ones'])
        idb = self.idb
        srcs = {'r': self.d['r16'], 'kk': self.d['kk16'], 'kz': [self.d['k0'], self.d['k1']],
                'bz': [self.d['b0'], self.d['b1']], 'sg': [self.d['sg0'], self.d['sg1']]}
        vtv = self.d['vtok'].rearrange("(n p) f -> p n f", p=128)
        yout = [self.d['yF'], self.d['yB']]

        def quad(s, hp):
            P = f'q{s}_'
            rows = slice(hp * 128, (hp + 1) * 128)
            if DB:
                inb = [[{n: self.alloc([NCB * 128], BF16) for n in ('r', 'kk', 'kz', 'bz')} for _ in range(2)] for _ in range(2)]
                sgb = [[self.alloc([NCB * 128], F32) for _ in range(2)] for _ in range(2)]
                vtb = [[self.alloc([NCB, 128], BF16) for _ in range(2)] for _ in range(2)]
            else:
                inb = [[{n: self.alloc([NCB * 128], BF16) for n in ('r', 'kk', 'kz', 'bz')}] * 2 for _ in range(2)]
                sgb = [[self.alloc([NCB * 128], F32)] * 2 for _ in range(2)]
                vtb = [[self.alloc([NCB, 128], BF16)] * 2 for _ in range(2)]
            G = [self.alloc([NCB * 128 + 1], F32) for _ in range(2)]
            for z in range(2):
                self.memset(G[z][:, 0:1], 0.0, [P + f'G{z}'])
            g4 = self.alloc([2, 2, 128], F32)
            e4 = g4
            eN = self.alloc([2, 128], F32)
            AR = [self.alloc([2, 256], BF16) for _ in range(2)]
            BT = [self.alloc([2, 128], BF16) for _ in range(2)]
            KT = [self.alloc([2, 128], BF16) for _ in range(2)]
            BH = self.alloc([2, 128], BF16)
            KH = self.alloc([2, 128], BF16)
            NM = self.alloc([4, 256], BF16)
            LK = self.alloc([4, 256], BF16)
            NI = [self.alloc([4, 128], BF16) for _ in range(7)]
            Np = [self.alloc([4, 128], BF16) for _ in range(2)]
            Gp = [self.alloc([4, 128], BF16) for _ in range(2)]
            BK = self.alloc([2, 256], BF16)
            X = [self.alloc([4, 64], BF16) for _ in range(2)]
            Sf = self.alloc([2, 64], F32)
            Sb = [self.alloc([2, 64], BF16) for _ in range(2)]
            yo = self.alloc([2, 128], F32)
            self.memset(Sf, 0.0, [P + 'Sf'])
            for hh in range(2):
                self.memset(Sb[hh], 0.0, [P + f'Sb{hh}'])
            par = [0, 0]
            for k in range(NC):
                ch = [k, NC - 1 - k]
                cbi = [ch[0] % NCB, ch[1] % NCB]
                col0 = [cbi[0] * 128, cbi[1] * 128]
                for z in range(2):
                    enter = (cbi[z] == 0) if z == 0 else (cbi[z] == NCB - 1)
                    if enter:
                        par[z] ^= DB
                        p = par[z]
                        blk = ch[z] // NCB
                        cs = slice(blk * NCB * 128, (blk + 1) * NCB * 128)
                        bk = P + f'in{z}{p}'
                        q_ = 'sp' if z == 0 else 'pool'
                        self.dma(q_, inb[z][p]['r'], srcs['r'][rows, cs], (), [bk])
                        self.dma(q_, inb[z][p]['kk'], srcs['kk'][rows, cs], (), [bk])
                        self.dma(q_, inb[z][p]['kz'], srcs['kz'][z][rows, cs], (), [bk])
                        self.dma(q_, inb[z][p]['bz'], srcs['bz'][z][rows, cs], (), [bk])
                        self.dma(q_, sgb[z][p], srcs['sg'][z][rows, cs], (), [bk])
                        self.dma(q_, vtb[z][p], vtv[:, blk * NCB:(blk + 1) * NCB, rows], (), [bk])
                        self.S.add('dve', lambda e, o=G[z], i=sgb[z][p]: e.tensor_tensor_scan(
                            out=o[:, 1:NCB * 128 + 1], data0=ones[:, 0:NCB * 128], data1=i, initial=0.0,
                            op0=ALU.mult, op1=ALU.add), [bk, 'ones'], [P + f'G{z}'])
                yield
                for z in range(2):
                    c0 = col0[z]
                    Gz = G[z]
                    gk = P + f'G{z}'
                    if z == 0:
                        self.ts(g4[:, z, 0, :], Gz[:, c0 + 1:c0 + 129], Gz[:, c0:c0 + 1], ALU.subtract, [gk], [P + 'g4', P + 'e4'], s2=-ADEC, op1=ALU.mult)
                        self.ts(g4[:, z, 1, :], Gz[:, c0:c0 + 128], Gz[:, c0:c0 + 1], ALU.subtract, [gk], [P + 'g4', P + 'e4'], s2=-ADEC, op1=ALU.mult)
                    else:
                        self.ts(g4[:, z, 0, :], Gz[:, c0:c0 + 128], Gz[:, c0 + 128:c0 + 129], ALU.subtract, [gk], [P + 'g4', P + 'e4'], s2=ADEC, op1=ALU.mult)
                        self.ts(g4[:, z, 1, :], Gz[:, c0 + 1:c0 + 129], Gz[:, c0 + 128:c0 + 129], ALU.subtract, [gk], [P + 'g4', P + 'e4'], s2=ADEC, op1=ALU.mult)
                self.act(eN, g4[:, :, 0, :], AF.Exp, [P + 'g4'], [P + 'eN'], scale=-1.0)
                self.act(e4, g4, AF.Exp, [P + 'g4'], [P + 'g4', P + 'e4'])
                lam = [e4[:, 0, 0, 127:128], e4[:, 1, 0, 0:1]]
                for z in range(2):
                    p = par[z]
                    cs = slice(col0[z], col0[z] + 128)
                    bk = P + f'in{z}{p}'
                    ib = inb[z][p]
                    for hh in range(2):
                        self.stt(AR[hh][:, z, 0:128], ib['kk'][:, cs], nm[:, hh:hh + 1], e4[:, z, 1, :], ALU.mult, ALU.mult,
                                 [bk, 'nm', P + 'e4'], [P + f'AR{hh}'])
                        self.stt(AR[hh][:, z, 128:256], ib['r'][:, cs], pm[:, hh:hh + 1], e4[:, z, 0, :], ALU.mult, ALU.mult,
                                 [bk, 'pm', P + 'e4'], [P + f'AR{hh}'])
                        self.stt(BT[hh][:, z, :], ib['bz'][:, cs], pm[:, hh:hh + 1], eN[:, z, :], ALU.mult, ALU.mult,
                                 [bk, 'pm', P + 'eN'], [P + f'BT{hh}'])
                        self.stt(KT[hh][:, z, :], ib['kz'][:, cs], pm[:, hh:hh + 1], eN[:, z, :], ALU.mult, ALU.mult,
                                 [bk, 'pm', P + 'eN'], [P + f'KT{hh}'])
                    self.stt(BH[:, z, :], ib['bz'][:, cs], lam[z], eN[:, z, :], ALU.mult, ALU.mult,
                             [bk, P + 'e4', P + 'eN'], [P + 'BH'])
                    self.stt(KH[:, z, :], ib['kz'][:, cs], lam[z], eN[:, z, :], ALU.mult, ALU.mult,
                             [bk, P + 'e4', P + 'eN'], [P + 'KH'])
                yield
                if STG < 2:
                    continue
                psA = [self.nb() for _ in range(2)]
                psB = [self.nb() for _ in range(2)]
                psC = self.nb()
                for q in range(4):
                    z, hh = q // 2, q % 2
                    self.mm(psA[z][0][:, hh * 256:(hh + 1) * 256], BT[hh][:, z, :], AR[hh][:, z, :], True, True,
                            [P + f'BT{hh}', P + f'AR{hh}'], [psA[z][1]])
                    self.mm(psB[z][0][:, hh * 256:(hh + 1) * 256], KT[hh][:, z, :], AR[hh][:, z, :], True, True,
                            [P + f'KT{hh}', P + f'AR{hh}'], [psB[z][1]])
                    self.mm(psC[0][:, q * 128:(q + 1) * 128], AR[hh][:, z, 0:128], BT[hh][:, z, :], True, True,
                            [P + f'BT{hh}', P + f'AR{hh}'], [psC[1]])
                for z in range(2):
                    self.tt(NM[:, 2 * z:2 * z + 2, :], psA[z][0][:, :].rearrange("p (a b) -> p a b", a=2), mAB[:, 2 * z:2 * z + 2, :],
                            ALU.mult, [psA[z][1], 'mAB'], [P + 'NM'])
                    self.tt(LK[:, 2 * z:2 * z + 2, :], psB[z][0][:, :].rearrange("p (a b) -> p a b", a=2), mAB[:, 2 * z:2 * z + 2, :],
                            ALU.mult, [psB[z][1], 'mAB'], [P + 'LK'])
                self.tt(Gp[0], psC[0][:, :].rearrange("p (a b) -> p a b", a=4), mC, ALU.mult, [psC[1], 'mC'], [P + 'Gp0'])
                self.tt(NI[0], NM[:, :, 0:128], id4, ALU.add, [P + 'NM', 'id4'], [P + 'NI0'], eng='pool')
                yield
                if STG < 3:
                    continue
                Ncur, Nk = NM[:, :, 0:128], P + 'NM'
                Gcur, Gk = Gp[0], P + 'Gp0'
                for i in range(6):
                    psN = self.nb()
                    for q in range(4):
                        self.mm(psN[0][:, q * 128:(q + 1) * 128], Gcur[:, q, :], Ncur[:, q, :], True, True, [Gk, Nk], [psN[1]])
                    if i < 5:
                        psG = self.nb()
                        for q in range(4):
                            self.mm(psG[0][:, q * 128:(q + 1) * 128], Ncur[:, q, :], Gcur[:, q, :], True, True, [Gk, Nk], [psG[1]])
                    pv = psN[0][:, :].rearrange("p (a b) -> p a b", a=4)
                    if i < 5:
                        nn = Np[i % 2]
                        nk = P + f'Np{i % 2}'
                        self.act(nn, pv, AF.Copy, [psN[1]], [nk])
                        self.tt(NI[i + 1], nn, id4, ALU.add, [nk, 'id4'], [P + f'NI{i + 1}'], eng='pool')
                    else:
                        self.tt(NI[i + 1], pv, id4, ALU.add, [psN[1], 'id4'], [P + f'NI{i + 1}'])
                    if i < 5:
                        gg = Gp[(i + 1) % 2]
                        ggk = P + f'Gp{(i + 1) % 2}'
                        self.act(gg, psG[0][:, :].rearrange("p (a b) -> p a b", a=4), AF.Copy, [psG[1]], [ggk])
                        Ncur, Nk, Gcur, Gk = nn, nk, gg, ggk
                    yield
                if STG < 4:
                    continue
                psT = self.nb()
                for z in range(2):
                    self.mm(psT[0][:, z * 256:z * 256 + 128], BH[:, z, :], idb, True, True, [P + 'BH', 'idb'], [psT[1]])
                    self.mm(psT[0][:, z * 256 + 128:z * 256 + 256], KH[:, z, :], idb, True, True, [P + 'KH', 'idb'], [psT[1]])
                self.act(BK, psT[0][:, :].rearrange("p (a b) -> p a b", a=2), AF.Copy, [psT[1]], [P + 'BK'])
                if k == NC // 2:
                    self.ts(Sf, Sf, self.flg[:, 0:1], ALU.mult, [P + 'Sf', 'flg'], [P + 'Sf'])
                    for hh in range(2):
                        self.act(Sb[hh], Sf, AF.Copy, [P + 'Sf', 'pm'], [P + f'Sb{hh}'], scale=pm[:, hh:hh + 1])
                if STG < 5:
                    continue
                psX = self.nb()
                for q in range(4):
                    z, hh = q // 2, q % 2
                    vt_ = vtb[z][par[z]][:, cbi[z], hh * 64:(hh + 1) * 64]
                    self.mm(psX[0][:, q * 64:(q + 1) * 64], AR[hh][:, z, 0:128], Sb[hh][:, z, :], True, False,
                            [P + f'AR{hh}', P + f'Sb{hh}'], [psX[1]])
                    self.mm(psX[0][:, q * 64:(q + 1) * 64], LK[:, q, 0:128], vt_, False, True,
                            [P + 'LK', P + f'in{z}{par[z]}'], [psX[1]])
                xi = 0
                self.act(X[0], psX[0][:, 0:256].rearrange("p (a b) -> p a b", a=4), AF.Copy, [psX[1]], [P + 'X0'])
                yield
                if STG < 6:
                    continue
                for i in range(7):
                    psX = self.nb()
                    for q in range(4):
                        self.mm(psX[0][:, q * 64:(q + 1) * 64], NI[i][:, q, :], X[xi][:, q, :], True, True,
                                [P + f'NI{i}', P + f'X{xi}'], [psX[1]])
                    xn = (xi + 1) % 2
                    self.evac(i, X[xn], psX[0][:, 0:256].rearrange("p (a b) -> p a b", a=4), [psX[1]], [P + f'X{xn}'])
                    xi = xn
                    yield
                U, Uk = X[xi], P + f'X{xi}'
                if STG < 7:
                    continue
                psY = self.nb()
                for q in range(4):
                    z, hh = q // 2, q % 2
                    vt_ = vtb[z][par[z]][:, cbi[z], hh * 64:(hh + 1) * 64]
                    o_ = psY[0][64 * hh:64 * hh + 64, z * 128:(z + 1) * 128]
                    self.mm(o_, Sb[hh][:, z, :], AR[hh][:, z, 128:256], True, False, [P + f'Sb{hh}', P + f'AR{hh}'], [psY[1]])
                    self.mm(o_, U[:, q, :], NM[:, q, 128:256], False, False, [Uk, P + 'NM'], [psY[1]])
                    self.mm(o_, vt_, LK[:, q, 128:256], False, True, [P + f'in{z}{par[z]}', P + 'LK'], [psY[1]])
                self.act(yo, psY[0][:, 0:256].rearrange("p (a b) -> p a b", a=2), AF.Copy, [psY[1]], [P + 'yo'])
                for z in range(2):
                    self.dma('sp', yout[z][rows, ch[z] * 128:(ch[z] + 1) * 128], yo[:, z, :], [P + 'yo'], ())
                if STG < 8:
                    continue
                psS = self.nb()
                for q in range(4):
                    z, hh = q // 2, q % 2
                    vt_ = vtb[z][par[z]][:, cbi[z], hh * 64:(hh + 1) * 64]
                    o_ = psS[0][64 * hh:64 * hh + 64, z * 64:(z + 1) * 64]
                    self.mm(o_, BK[:, z, hh * 64:(hh + 1) * 64], U[:, q, :], True, False, [P + 'BK', Uk], [psS[1]])
                    self.mm(o_, BK[:, z, 128 + hh * 64:128 + (hh + 1) * 64], vt_, False, True, [P + 'BK', P + f'in{z}{par[z]}'], [psS[1]])
                for z in range(2):
                    self.stt(Sf[:, z, :], Sf[:, z, :], lam[z], psS[0][:, z * 64:(z + 1) * 64], ALU.mult, ALU.add,
                             [P + 'Sf', P + 'e4', psS[1]], [P + 'Sf'])
                for hh in range(2):
                    self.act(Sb[hh], Sf, AF.Copy, [P + 'Sf', 'pm'], [P + f'Sb{hh}'], scale=pm[:, hh:hh + 1])
                yield

        mark0 = self.off
        for hp0 in range(0, 8, NQ):
            self.off = mark0
            gens = [quad(j, hp0 + j) for j in range(min(NQ, 8 - hp0))]
            alive = [True] * len(gens)
            while any(alive):
                for gi_, g_ in enumerate(gens):
                    if alive[gi_]:
                        try:
                            next(g_)
                        except StopIteration:
                            alive[gi_] = False
            self.S.barrier()
        self.off = mark0
        bonf = cst[:, C_BON:C_BON + 128]
        NB_ = 2
        yf = [self.alloc([512], F32) for _ in range(NB_)]
        yb = [self.alloc([512], F32) for _ in range(NB_)]
        bo = [self.alloc([512], BF16) for _ in range(NB_)]
        gt = [self.alloc([512], BF16) for _ in range(NB_)]
        sq = [self.alloc([512], F32) for _ in range(NB_)]
        mean = [self.alloc([512], F32) for _ in range(NB_)]
        m2 = [self.alloc([512], F32) for _ in range(NB_)]
        oo = [self.alloc([512], BF16) for _ in range(NB_)]
        it = 0
        for hp in range(8):
            rows = slice(hp * 128, (hp + 1) * 128)
            for i in range(self.NT):
                b = it % NB_
                it += 1
                sl = slice(i * 512, (i + 1) * 512)
                self.dma('sp', yf[b], self.d['yF'][rows, sl], (), [f'yf{b}'])
                self.dma('sp', yb[b], self.d['yB'][rows, sl], (), [f'yb{b}'])
                self.dma('sp', bo[b], self.d['bonus16'][rows, sl], (), [f'bo{b}'])
                self.dma('sp', gt[b], self.d['gate16'][rows, sl], (), [f'gt{b}'])
                self.tt(yf[b], yf[b], yb[b], ALU.add, [f'yf{b}', f'yb{b}'], [f'yf{b}'])
                self.act(sq[b], yf[b], AF.Square, [f'yf{b}'], [f'sq{b}'])
                psM = self.nb()
                psQ = self.nb()
                self.mm(psM[0][:, :], bonf, yf[b], True, True, ['cst', f'yf{b}'], [psM[1]])
                self.mm(psQ[0][:, :], bonf, sq[b], True, True, ['cst', f'sq{b}'], [psQ[1]])
                self.act(mean[b], psM[0][:, :], AF.Copy, [psM[1]], [f'mean{b}'], scale=1.0 / 64)
                self.tt(m2[b], mean[b], mean[b], ALU.mult, [f'mean{b}'], [f'm2{b}'])
                self.stt(m2[b], psQ[0][:, :], 1.0 / 64, m2[b], ALU.mult, ALU.subtract, [psQ[1], f'm2{b}'], [f'm2{b}'])
                self.ts(m2[b], m2[b], GN_EPS, ALU.add, [f'm2{b}'], [f'm2{b}'])
                self.act(m2[b], m2[b], AF.Sqrt, [f'm2{b}'], [f'm2{b}'])
                self.S.add('dve', lambda e, t_=m2[b]: e.reciprocal(out=t_, in_=t_), [f'm2{b}'], [f'm2{b}'])
                self.tt(yf[b], yf[b], mean[b], ALU.subtract, [f'yf{b}', f'mean{b}'], [f'yf{b}'])
                self.tt(yf[b], yf[b], m2[b], ALU.mult, [f'yf{b}', f'm2{b}'], [f'yf{b}'])
                self.act(yf[b], yf[b], AF.Identity, [f'yf{b}', 'par'], [f'yf{b}'],
                         bias=self.pcol('lnx_b', hp), scale=self.pcol('lnx_g', hp))
                self.tt(yf[b], yf[b], bo[b], ALU.add, [f'yf{b}', f'bo{b}'], [f'yf{b}'])
                self.tt(oo[b], yf[b], gt[b], ALU.mult, [f'yf{b}', f'gt{b}'], [f'oo{b}'])
                self.dma('sp', self.d['out16'][rows, sl], oo[b], [f'oo{b}'], ())

    def phase_moe(self):
        T = self.T
        CAP, CAP1 = self.CAP, self.CAP1
        NTK = T // 128
        self.psi = 0
        cst = self.cst
        sl_all = self.alloc([NTK, 2], I32)
        g_all = self.alloc([NTK, 2], F32)
        cntt = self.alloc([8], F32)
        self.memset(cntt, 0.0, ['cnt'])
        mark1 = self.off
        zt = self.alloc([4096], BF16)
        self.memset(zt, 0.0, ['zt'])
        nrow = NE * CAP1
        xgf = self.d['Xg'].rearrange("(p n) f -> p (n f)", p=128)
        tot = nrow // 128 * D
        for a in range(0, tot, 4096):
            b_ = min(tot, a + 4096)
            self.dma('sp', xgf[:, a:b_], zt[:, 0:b_ - a], ['zt'], ['Xg'])
        self.S.barrier()
        wr = self.alloc([8, 8], F32)
        self.dma('sp', wr, self.d['moe_w_router'].rearrange("(c p) n -> p c n", p=128), (), ['wr'])
        brt = self.alloc([8], F32)
        self.dma('sp', brt, self.d['brt'], (), ['brt'])
        xg = [self.alloc([8, 512], F32) for _ in range(2)]
        xtok = [self.alloc([D], F32) for _ in range(2)]
        xtb = [self.alloc([D], BF16) for _ in range(2)]
        NS_ = 3
        sm = [{n: self.alloc([8], F32) for n in ('lg', 'eq1', 'lg2', 'eq2', 'mask', 'pos', 't1', 'sv')} for _ in range(NS_)]
        sc = [{n: self.alloc([1], F32) for n in ('m1', 'm2', 'nm1', 's1', 's2')} for _ in range(NS_)]
        x2v = self.d['x2'].rearrange("(c p) t -> p c t", p=128)
        x2t = self.d['x2tok'].rearrange("(n p) f -> n p f", p=128)
        ident = cst[:, C_I:C_I + 128]
        onesf = cst[:, C_ONE:C_ONE + 128]
        mus = cst[:, C_MUS:C_MUS + 128]
        ebase = cst[:, C_EB:C_EB + 8]
        Xg = self.d['Xg']
        Yg = self.d['Yg']
        for i in range(self.NT):
            b = i % 2
            xgk = f'xg{b}'
            self.dma('sp', xg[b], x2v[:, :, i * 512:(i + 1) * 512], (), [xgk])
            for st_ in range(4):
                n = i * 4 + st_
                s3 = n % NS_
                t2 = n % 2
                S_ = sm[s3]
                C_ = sc[s3]
                K_ = f's{s3}'
                ts_ = slice(st_ * 128, (st_ + 1) * 128)
                psR = self.nb()
                for kc in range(8):
                    self.mm(psR[0][:, 0:8], xg[b][:, kc, ts_], wr[:, kc, :], kc == 0, kc == 7, [xgk, 'wr'], [psR[1]])
                self.tt(S_['lg'], psR[0][:, 0:8], brt, ALU.add, [psR[1], 'brt'], [K_ + 'lg'])
                self.S.add('dve', lambda e, o=C_['m1'], i_=S_['lg']: e.tensor_reduce(out=o, in_=i_, axis=AX.X, op=ALU.max), [K_ + 'lg'], [K_ + 'm1'])
                self.ts(S_['eq1'], S_['lg'], C_['m1'], ALU.is_equal, [K_ + 'lg', K_ + 'm1'], [K_ + 'eq1'])
                self.stt(S_['lg2'], S_['eq1'], -1e30, S_['lg'], ALU.mult, ALU.add, [K_ + 'eq1', K_ + 'lg'], [K_ + 'lg2'])
                self.S.add('dve', lambda e, o=C_['m2'], i_=S_['lg2']: e.tensor_reduce(out=o, in_=i_, axis=AX.X, op=ALU.max), [K_ + 'lg2'], [K_ + 'm2'])
                self.ts(S_['eq2'], S_['lg2'], C_['m2'], ALU.is_equal, [K_ + 'lg2', K_ + 'm2'], [K_ + 'eq2'])
                self.ts(C_['nm1'], C_['m1'], -1.0, ALU.mult, [K_ + 'm1'], [K_ + 'nm1'])
                self.act(g_all[:, n, 1:2], C_['m2'], AF.Sigmoid, [K_ + 'm2', K_ + 'nm1'], ['gall'], bias=C_['nm1'])
                self.act(g_all[:, n, 0:1], C_['m2'], AF.Sigmoid, [K_ + 'm2', K_ + 'm1'], ['gall'], bias=C_['m1'], scale=-1.0)
                self.tt(S_['mask'], S_['eq1'], S_['eq2'], ALU.add, [K_ + 'eq1', K_ + 'eq2'], [K_ + 'mask'])
                psP = self.nb()
                self.mm(psP[0][:, 0:8], mus, S_['mask'], True, True, ['cst', K_ + 'mask'], [psP[1]])
                self.mm(psP[0][:, 8:16], onesf, S_['mask'], True, True, ['cst', K_ + 'mask'], [psP[1]])
                self.tt(S_['pos'], psP[0][:, 0:8], cntt, ALU.add, [psP[1], 'cnt'], [K_ + 'pos'])
                self.tt(cntt, cntt, psP[0][:, 8:16], ALU.add, [psP[1], 'cnt'], ['cnt'])
                self.ts(S_['pos'], S_['pos'], float(CAP), ALU.min, [K_ + 'pos'], [K_ + 'pos'])
                self.tt(S_['t1'], S_['pos'], ebase, ALU.add, [K_ + 'pos', 'cst'], [K_ + 't1'])
                for kk_, eqn in enumerate(('eq1', 'eq2')):
                    self.tt(S_['sv'], S_['t1'], S_[eqn], ALU.mult, [K_ + 't1', K_ + eqn], [K_ + 'sv'])
                    sn = 's1' if kk_ == 0 else 's2'
                    self.S.add('dve', lambda e, o=C_[sn], i_=S_['sv']: e.tensor_reduce(out=o, in_=i_, axis=AX.X, op=ALU.add), [K_ + 'sv'], [K_ + sn])
                    self.cp(sl_all[:, n, kk_:kk_ + 1], C_[sn], [K_ + sn], ['slall'])
                pst = [self.nb(), self.nb()]
                for kc in range(8):
                    p_ = pst[kc // 4]
                    self.mm(p_[0][:, (kc % 4) * 128:(kc % 4 + 1) * 128], xg[b][:, kc, ts_], ident, True, True, [xgk, 'cst'], [p_[1]])
                for h2 in range(2):
                    self.evac(h2, xtok[t2][:, h2 * 512:(h2 + 1) * 512], pst[h2][0][:, :], [pst[h2][1]], [f'xtok{t2}_{h2}'])
                    self.cp(xtb[t2][:, h2 * 512:(h2 + 1) * 512], xtok[t2][:, h2 * 512:(h2 + 1) * 512], [f'xtok{t2}_{h2}'], [f'xtb{t2}'], eng='pool')
                self.dma('sp', x2t[n], xtok[t2], [f'xtok{t2}_0', f'xtok{t2}_1'], ())
                for kk_ in range(2):
                    self.S.add('pool', lambda e, o=sl_all[:, n, kk_:kk_ + 1], src=xtb[t2]: e.indirect_dma_start(
                        out=Xg, out_offset=bass.IndirectOffsetOnAxis(ap=o, axis=0), in_=src, in_offset=None),
                        [f'xtb{t2}', 'slall', 'Xg'], ['Xgw'], is_dma=True)
        self.S.barrier()
        self.off = mark1
        NST = CAP // 128
        XT = self.alloc([8, CAP], BF16)
        acc = self.alloc([NST, D], F32)
        xs = [self.alloc([D], BF16) for _ in range(2)]
        FG = 256
        NFG = DEX // FG
        wgs = [self.alloc([8, FG], BF16) for _ in range(2)]
        wus = [self.alloc([8, FG], BF16) for _ in range(2)]
        wds = [self.alloc([2, D], BF16) for _ in range(2)]
        hT = self.alloc([2, CAP], BF16)
        sgt = [self.alloc([512], BF16) for _ in range(2)]
        idb = self.idb
        wi = 0
        ei = 0
        for ex in range(NE):
            r0 = ex * CAP1
            for st_ in range(NST):
                b = st_ % 2
                self.dma('sp', xs[b], Xg[r0 + st_ * 128:r0 + (st_ + 1) * 128, :], (), [f'xs{b}'])
                pst = [self.nb(), self.nb()]
                for kc in range(8):
                    p_ = pst[kc // 4]
                    self.mm(p_[0][:, (kc % 4) * 128:(kc % 4 + 1) * 128], xs[b][:, kc * 128:(kc + 1) * 128], idb, True, True,
                            [f'xs{b}', 'idb'], [p_[1]])
                for h2 in range(2):
                    self.evac(h2, XT[:, h2 * 4:(h2 + 1) * 4, st_ * 128:(st_ + 1) * 128],
                              pst[h2][0][:, :].rearrange("p (a b) -> p a b", a=4), [pst[h2][1]], [f'XT{st_}'])
            gv = self.d['moe_w_gate'][ex].rearrange("(c p) f -> p c f", p=128)
            uv = self.d['moe_w_up'][ex].rearrange("(c p) f -> p c f", p=128)
            dv = self.d['moe_w_down'][ex].rearrange("(c p) n -> p c n", p=128)
            for fg in range(NFG):
                wb = wi % 2
                wi += 1
                self.dma('pool', wgs[wb], gv[:, :, fg * FG:(fg + 1) * FG], (), [f'wgs{wb}'])
                self.dma('pool', wus[wb], uv[:, :, fg * FG:(fg + 1) * FG], (), [f'wus{wb}'])
                self.dma('pool', wds[wb], dv[:, fg * 2:fg * 2 + 2, :], (), [f'wds{wb}'])
                for sub in range((CAP + 511) // 512):
                    c0 = sub * 512
                    ns = min(512, CAP - c0)
                    xk_ = [f'XT{j}' for j in range(c0 // 128, (c0 + ns) // 128)]
                    for fc in range(2):
                        psg = self.nb()
                        psu = self.nb()
                        for kc in range(8):
                            self.mm(psg[0][:, 0:ns], wgs[wb][:, kc, fc * 128:(fc + 1) * 128], XT[:, kc, c0:c0 + ns], kc == 0, kc == 7,
                                    [f'wgs{wb}'] + xk_, [psg[1]])
                        for kc in range(8):
                            self.mm(psu[0][:, 0:ns], wus[wb][:, kc, fc * 128:(fc + 1) * 128], XT[:, kc, c0:c0 + ns], kc == 0, kc == 7,
                                    [f'wus{wb}'] + xk_, [psu[1]])
                        s2 = ei % 2
                        ei += 1
                        self.act(sgt[s2][:, 0:ns], psg[0][:, 0:ns], AF.Silu, [psg[1]], [f'sgt{s2}'])
                        self.tt(hT[:, fc, c0:c0 + ns], sgt[s2][:, 0:ns], psu[0][:, 0:ns], ALU.mult, [f'sgt{s2}', psu[1]], [f'hT{sub}_{fc}'])
                for st_ in range(NST):
                    sub = st_ // 4
                    for half in range(2):
                        psd = self.nb()
                        for fc in range(2):
                            self.mm(psd[0][:, :], hT[:, fc, st_ * 128:(st_ + 1) * 128], wds[wb][:, fc, half * 512:(half + 1) * 512],
                                    fc == 0, fc == 1, [f'hT{sub}_{fc}', f'wds{wb}'], [psd[1]])
                        ak = f'acc{st_}_{half}'
                        if fg == 0:
                            self.act(acc[:, st_, half * 512:(half + 1) * 512], psd[0][:, :], AF.Copy, [psd[1]], [ak])
                        else:
                            self.tt(acc[:, st_, half * 512:(half + 1) * 512], acc[:, st_, half * 512:(half + 1) * 512], psd[0][:, :],
                                    ALU.add, [ak, psd[1]], [ak])
            self.dma('sp', Yg[r0:r0 + CAP, :].rearrange("(n p) f -> p n f", p=128), acc,
                     [f'acc{j}_{h}' for j in range(NST) for h in range(2)], ['Yg'])
        self.S.barrier()
        self.off = mark1
        lng = self.alloc([D], F32)
        lnb = self.alloc([D], F32)
        self.dma('sp', lng, self.d['lnf'][0], (), ['lng'])
        self.dma('sp', lnb, self.d['lnf'][1], (), ['lnb'])
        Y1 = [self.alloc([D], F32) for _ in range(2)]
        Y2 = [self.alloc([D], F32) for _ in range(2)]
        xt = [self.alloc([D], F32) for _ in range(2)]
        jk = [self.alloc([D], F32) for _ in range(2)]
        st4 = [{n: self.alloc([1], F32) for n in ('ss', 'sq', 'mean', 'm2', 'rstd')} for _ in range(2)]
        for n in range(NTK):
            b = n % 2
            for kk_, Yt in enumerate((Y1, Y2)):
                self.S.add('pool', lambda e, o=Yt[b], ix=sl_all[:, n, kk_:kk_ + 1]: e.indirect_dma_start(
                    out=o, out_offset=None, in_=Yg, in_offset=bass.IndirectOffsetOnAxis(ap=ix, axis=0)),
                    ['slall2'], [f'Y{kk_}{b}'], is_dma=True)
            self.dma('sp', xt[b], x2t[n], (), [f'xt{b}'])
            s_ = st4[b]
            K_ = f'f{b}'
            self.act(xt[b], xt[b], AF.Copy, [f'xt{b}'], [f'xt{b}'], scale=ALPHA)
            self.stt(xt[b], Y1[b], g_all[:, n, 0:1], xt[b], ALU.mult, ALU.add, [f'Y0{b}', f'xt{b}', 'gall2'], [f'xt{b}'])
            self.stt(xt[b], Y2[b], g_all[:, n, 1:2], xt[b], ALU.mult, ALU.add, [f'Y1{b}', f'xt{b}', 'gall2'], [f'xt{b}'])
            self.act(jk[b], xt[b], AF.Identity, [f'xt{b}'], [f'jk{b}', K_ + 'ss'], accum=s_['ss'])
            self.act(jk[b], xt[b], AF.Square, [f'xt{b}'], [f'jk{b}', K_ + 'sq'], accum=s_['sq'])
            self.ts(s_['mean'], s_['ss'], 1.0 / D, ALU.mult, [K_ + 'ss'], [K_ + 'mean'])
            self.tt(s_['m2'], s_['mean'], s_['mean'], ALU.mult, [K_ + 'mean'], [K_ + 'm2'])
            self.stt(s_['m2'], s_['sq'], 1.0 / D, s_['m2'], ALU.mult, ALU.subtract, [K_ + 'sq', K_ + 'm2'], [K_ + 'm2'])
            self.ts(s_['m2'], s_['m2'], LN_EPS, ALU.add, [K_ + 'm2'], [K_ + 'm2'])
            self.act(s_['rstd'], s_['m2'], AF.Sqrt, [K_ + 'm2'], [K_ + 'rstd'])
            self.S.add('dve', lambda e, t_=s_['rstd']: e.reciprocal(out=t_, in_=t_), [K_ + 'rstd'], [K_ + 'rstd'])
            self.ts(xt[b], xt[b], s_['mean'], ALU.subtract, [f'xt{b}', K_ + 'mean', K_ + 'rstd'], [f'xt{b}'], s2=s_['rstd'], op1=ALU.mult)
            self.tt(xt[b], xt[b], lng, ALU.mult, [f'xt{b}', 'lng'], [f'xt{b}'])
            self.tt(jk[b], xt[b], lnb, ALU.add, [f'xt{b}', 'lnb'], [f'jk{b}'], eng='pool')
            self.dma('sp', self.d['y'][n * 128:(n + 1) * 128, :], jk[b], [f'jk{b}'], ())


def pack_params(inp):
    def col(v):
        return np.ascontiguousarray(np.asarray(v, np.float32).reshape(8, 128).T)
    p = np.zeros((128, NPCOL), np.float32)
    src = {
        'ln_mix_g0': inp['ln_mix_g'][0], 'ln_mix_b0': inp['ln_mix_b'][0],
        'ln_ffn_g0': inp['ln_ffn_g'][0], 'ln_ffn_b0': inp['ln_ffn_b'][0],
        'ln_mix_g1': inp['ln_mix_g'][1], 'ln_mix_b1': inp['ln_mix_b'][1],
        'w0_0': inp['rwkv_w0'][0, 0], 'w0_1': inp['rwkv_w0'][0, 1],
        'a0_0': inp['rwkv_a0'][0, 0], 'a0_1': inp['rwkv_a0'][0, 1],
        'k_k': inp['rwkv_k_k'][0], 'k_a': inp['rwkv_k_a'][0], 'r_k': inp['rwkv_r_k'][0].reshape(-1),
        'lnx_g': inp['rwkv_lnx_g'][0], 'lnx_b': inp['rwkv_lnx_b'][0],
    }
    for i in range(6):
        src[f'mu{i}'] = inp['rwkv_mu'][0, i]
    for n, v in src.items():
        p[:, PCOL[n]:PCOL[n] + 8] = col(v)
    return p


def shared_inputs(inp, T):
    sh = {}
    sh['params'] = pack_params(inp)
    sh['consts'] = make_consts(moe_cap(T) + 128)
    sh['lnf'] = np.ascontiguousarray(np.stack([np.broadcast_to(inp['ln_ffn_g'][1][None, :], (128, D)),
                                               np.broadcast_to(inp['ln_ffn_b'][1][None, :], (128, D))]).astype(np.float32))
    sh['brt'] = np.ascontiguousarray(np.broadcast_to(inp['moe_b_router'][0][None, :], (128, 8)).astype(np.float32))
    for n in ('na_w_qkv', 'na_w_o', 'ffn_w_gate', 'ffn_w_up', 'ffn_w_down', 'rwkv_w_rkv', 'rwkv_w1', 'rwkv_w2',
              'rwkv_a1', 'rwkv_a2', 'rwkv_g1', 'rwkv_g2', 'rwkv_w_o', 'moe_w_router', 'moe_w_gate', 'moe_w_up', 'moe_w_down'):
        sh[n] = np.ascontiguousarray(np.asarray(inp[n], np.float32)[0])
    return sh


def core_inputs(sh, tabs, x_tok, ctype):
    m = dict(sh)
    m['xT'] = np.ascontiguousarray(np.asarray(x_tok, np.float32).T)
    m['natab'] = tabs[ctype]
    fl = np.zeros((128, 2), np.float32)
    fl[:, 0] = 1.0 if ctype == 'P' else 0.0
    m['flags'] = fl
    return m


_PROG = {}


def get_program(T, dbg=(), stop_after=None):
    k = (T, tuple(sorted(dbg)), stop_after)
    if k not in _PROG:
        b = Builder(T, dbg, stop_after)
        b.build()
        _PROG[k] = b
    return _PROG[k]


def kernel(**inputs):
    T = 8192
    inp = {k: np.asarray(v) for k, v in inputs.items()}
    b = get_program(T)
    sh = shared_inputs(inp, T)
    rpb = np.asarray(inp['na_rpb'], np.float32)[0]
    tabs = {c: np.ascontiguousarray(na_table(rpb, c, b.slots).reshape(32, 128, -1)) for c in 'PS'}
    xp = inp['x_prompt']
    xs = inp['x_sample']
    maps = []
    for c in range(8):
        if c < 2:
            maps.append(core_inputs(sh, tabs, xp[c], 'P'))
        elif c < 6:
            s = 2 * (c - 2)
            maps.append(core_inputs(sh, tabs, np.concatenate([xs[s], xs[s + 1]], 0), 'S'))
        else:
            maps.append(core_inputs(sh, tabs, np.zeros((T, D), np.float32), 'P'))
    res = run_bass_kernel_spmd(b.nc, maps, core_ids=list(range(8)))
    yp = np.stack([res.results[c]['y'] for c in range(2)]).astype(np.float32)
    ys = np.concatenate([res.results[c]['y'].reshape(2, T // 2, D) for c in range(2, 6)], 0).astype(np.float32)
    return (yp, ys)
```
